# Optimizing a Trainium2 kernel written in Bass

```python
import math
import jax, jax.numpy as jnp
from jax import lax
import numpy as np

D_MODEL = 1024
BATCH = 8
SEQ = 2048
DEPTH = 2
DEC_BATCH = 128
DEC_SEQ = 4
PAST_LEN = 16384
PAGE_SIZE = 128

GDN_HEADS = 4
GDN_DK = 128
GDN_DV = 128
GDN_K = GDN_HEADS * GDN_DK
GDN_V = GDN_HEADS * GDN_DV
CONV_W = 4
GDN_CONV_DIM = 2 * GDN_K + GDN_V
ML_HEADS = 4
ML_DK = 64
ML_DV = 128
ML_K = ML_HEADS * ML_DK
ML_V = ML_HEADS * ML_DV
D_FF = 4 * D_MODEL
CHUNK = 64
EPS = 1e-6
IN_SPLITS = (GDN_K, GDN_K, GDN_V, GDN_V, GDN_HEADS, GDN_HEADS,
             ML_K, ML_K, ML_V, ML_V, ML_HEADS, ML_HEADS, D_MODEL, D_MODEL)
IN_DIM = sum(IN_SPLITS)

kernel_name = 'hybrid_gdn_mlstm_gated_merge_step'


def _split_cols(a, sizes):
    offs = [int(o) for o in np.cumsum(sizes)[:-1]]
    return jnp.split(a, offs, axis=-1)


def _rmsnorm(x, w):
    xf = x.astype(jnp.float32)
    y = xf * lax.rsqrt(jnp.mean(xf * xf, axis=-1, keepdims=True) + EPS) * w.astype(jnp.float32)
    return y.astype(x.dtype)


def _head_rmsnorm(x, w):
    D = x.shape[-1]
    y = x * lax.rsqrt(jnp.mean(x * x, axis=-1, keepdims=True) + EPS)
    return y * w.astype(jnp.float32).reshape(-1, D)


def _l2norm(x):
    return x * lax.rsqrt(jnp.sum(x * x, axis=-1, keepdims=True) + EPS)


def _chunk_len(T):
    return CHUNK if T % CHUNK == 0 else T


def _to_chunks(a, L):
    B, T, H = a.shape[:3]
    a = a.reshape((B, T // L, L, H) + a.shape[3:])
    return jnp.moveaxis(a, (1, 3), (0, 2))


def _from_chunks(o):
    o = jnp.moveaxis(o, (0, 2), (1, 3))
    B, N, L, H, D = o.shape
    return o.reshape(B, N * L, H, D)


def _causal_conv_silu(x, buf, w):
    T = x.shape[1]
    xp = jnp.concatenate([buf.astype(x.dtype), x], axis=1)
    y = xp[:, 0:T] * w[0]
    for j in range(1, CONV_W):
        y = y + xp[:, j:j + T] * w[j]
    return jax.nn.silu(y), xp[:, -(CONV_W - 1):]


def _gated_delta_chunked(q, k, v, g, beta, S0):
    L = _chunk_len(q.shape[1])
    q, k, v, g, beta = (_to_chunks(a, L) for a in (q, k, v, g, beta))
    incl = jnp.tril(jnp.ones((L, L), bool))
    strict = jnp.tril(jnp.ones((L, L), bool), -1)
    gc = jnp.cumsum(g, axis=-1)
    decay = jnp.exp(jnp.where(incl, gc[..., :, None] - gc[..., None, :], -jnp.inf))
    kb = k * beta[..., None]
    A = jnp.where(strict, jnp.einsum('nbhid,nbhjd->nbhij', kb, k) * decay, 0.0)
    eye = jnp.broadcast_to(jnp.eye(L, dtype=A.dtype), A.shape)
    Tinv = lax.linalg.triangular_solve(A, eye, left_side=True, lower=True, unit_diagonal=True)
    w = jnp.einsum('nbhij,nbhjd->nbhid', Tinv, kb * jnp.exp(gc)[..., None])
    u = jnp.einsum('nbhij,nbhje->nbhie', Tinv, v * beta[..., None])
    qk = jnp.where(incl, jnp.einsum('nbhid,nbhjd->nbhij', q, k) * decay, 0.0)
    qd = q * jnp.exp(gc)[..., None]
    kd = k * jnp.exp(gc[..., -1:] - gc)[..., None]
    dl = jnp.exp(gc[..., -1])

    def step(S, xs):
        qd_c, qk_c, u_c, w_c, kd_c, dl_c = xs
        v_new = u_c - jnp.einsum('bhld,bhde->bhle', w_c, S)
        o = jnp.einsum('bhld,bhde->bhle', qd_c, S) + jnp.einsum('bhij,bhje->bhie', qk_c, v_new)
        S = S * dl_c[..., None, None] + jnp.einsum('bhld,bhle->bhde', kd_c, v_new)
        return S, o

    S, o = lax.scan(step, S0, (qd, qk, u, w, kd, dl))
    return _from_chunks(o), S


def _mlstm_chunked(q, k, v, log_i, log_f, C0, n0, m0):
    L = _chunk_len(q.shape[1])
    q, k, v, log_i, log_f = (_to_chunks(a, L) for a in (q, k, v, log_i, log_f))
    incl = jnp.tril(jnp.ones((L, L), bool))
    F = jnp.cumsum(log_f, axis=-1)
    D = jnp.where(incl, F[..., :, None] - F[..., None, :] + log_i[..., None, :], -jnp.inf)
    D_max = jnp.max(D, axis=-1)
    qk = jnp.einsum('nbhid,nbhjd->nbhij', q, k)
    G = log_i + F[..., -1:] - F
    G_max = jnp.max(G, axis=-1)

    def step(carry, xs):
        C, n, m = carry
        q_c, k_c, v_c, F_c, D_c, Dm_c, qk_c, G_c, Gm_c = xs
        inter = m[..., None] + F_c
        m_row = jnp.maximum(Dm_c, inter)
        Sm = jnp.exp(D_c - m_row[..., None]) * qk_c
        a = jnp.exp(inter - m_row)
        num = a[..., None] * jnp.einsum('bhld,bhde->bhle', q_c, C) + jnp.einsum('bhij,bhje->bhie', Sm, v_c)
        den = a * jnp.einsum('bhld,bhd->bhl', q_c, n) + jnp.sum(Sm, axis=-1)
        h = num / jnp.maximum(jnp.abs(den), jnp.exp(-m_row))[..., None]
        Fl = F_c[..., -1]
        m_new = jnp.maximum(m + Fl, Gm_c)
        dec = jnp.exp(m + Fl - m_new)
        wC = jnp.exp(G_c - m_new[..., None])
        C = dec[..., None, None] * C + jnp.einsum('bhl,bhld,bhle->bhde', wC, k_c, v_c)
        n = dec[..., None] * n + jnp.einsum('bhl,bhld->bhd', wC, k_c)
        return (C, n, m_new), h

    (C, n, m), h = lax.scan(step, (C0, n0, m0), (q, k, v, F, D, D_max, qk, G, G_max))
    return _from_chunks(h), C, n, m


def _mixer(h, conv_buf, S0, C0, n0, m0, w_in, gdn_conv_w, gdn_A_log, gdn_dt_bias, gdn_norm,
           ml_i_bias, ml_f_bias, ml_norm, w_branch_gdn, w_branch_ml, w_out):
    B, T, _ = h.shape
    f32 = jnp.float32
    (qg, kg, vg, zg, bg, ag, qm, km, vm, om, im, fm, gate_g, gate_m) = _split_cols(h @ w_in, IN_SPLITS)
    qkv, new_buf = _causal_conv_silu(jnp.concatenate([qg, kg, vg], axis=-1), conv_buf, gdn_conv_w)
    qg, kg, vg = _split_cols(qkv.astype(f32), (GDN_K, GDN_K, GDN_V))
    qg = _l2norm(qg.reshape(B, T, GDN_HEADS, GDN_DK)) * GDN_DK ** -0.5
    kg = _l2norm(kg.reshape(B, T, GDN_HEADS, GDN_DK))
    vg = vg.reshape(B, T, GDN_HEADS, GDN_DV)
    beta = jax.nn.sigmoid(bg.astype(f32))
    g = -jnp.exp(gdn_A_log.astype(f32)) * jax.nn.softplus(ag.astype(f32) + gdn_dt_bias.astype(f32))
    og, S_new = _gated_delta_chunked(qg, kg, vg, g, beta, S0.astype(f32))
    og = _head_rmsnorm(og, gdn_norm) * jax.nn.silu(zg.astype(f32)).reshape(B, T, GDN_HEADS, GDN_DV)
    br_g = og.reshape(B, T, GDN_V).astype(h.dtype) @ w_branch_gdn
    qm = qm.astype(f32).reshape(B, T, ML_HEADS, ML_DK) * ML_DK ** -0.5
    km = km.astype(f32).reshape(B, T, ML_HEADS, ML_DK)
    vm = vm.astype(f32).reshape(B, T, ML_HEADS, ML_DV)
    log_i = im.astype(f32) + ml_i_bias.astype(f32)
    log_f = jax.nn.log_sigmoid(fm.astype(f32) + ml_f_bias.astype(f32))
    hm, C_new, n_new, m_new = _mlstm_chunked(qm, km, vm, log_i, log_f,
                                            C0.astype(f32), n0.astype(f32), m0.astype(f32))
    hm = _head_rmsnorm(hm, ml_norm) * jax.nn.sigmoid(om.astype(f32)).reshape(B, T, ML_HEADS, ML_DV)
    br_m = hm.reshape(B, T, ML_V).astype(h.dtype) @ w_branch_ml
    merged = jax.nn.sigmoid(gate_g) * br_g + jax.nn.sigmoid(gate_m) * br_m
    dt = h.dtype
    return merged @ w_out, (new_buf.astype(dt), S_new.astype(dt), C_new.astype(dt), n_new.astype(dt), m_new.astype(dt))


def _trunk(x, conv_buf, gdn_S, ml_C, ml_n, ml_m, norm_mix, w_in, gdn_conv_w, gdn_A_log, gdn_dt_bias,
           gdn_norm, ml_i_bias, ml_f_bias, ml_norm, w_branch_gdn, w_branch_ml, w_out, norm_mlp,
           w_up, w_down, norm_final):
    new = ([], [], [], [], [])
    for l in range(DEPTH):
        mix, st = _mixer(_rmsnorm(x, norm_mix[l]), conv_buf[l], gdn_S[l], ml_C[l], ml_n[l], ml_m[l],
                         w_in[l], gdn_conv_w[l], gdn_A_log[l], gdn_dt_bias[l], gdn_norm[l],
                         ml_i_bias[l], ml_f_bias[l], ml_norm[l], w_branch_gdn[l], w_branch_ml[l], w_out[l])
        x = x + mix
        hmlp = _rmsnorm(x, norm_mlp[l])
        x = x + jnp.square(jax.nn.relu(hmlp @ w_up[l])) @ w_down[l]
        for lst, s in zip(new, st):
            lst.append(s)
    y = _rmsnorm(x, norm_final)
    conv_n, S_n, C_n, n_n, m_n = (jnp.stack(lst) for lst in new)
    return y, conv_n, S_n, C_n, n_n, m_n


def setup_inputs(seed: int = 0) -> dict:
    key = jax.random.key(seed)
    ks = jax.random.split(key, 24)
    f32 = jnp.float32

    def nrm(k, shape, scale):
        return jax.random.normal(k, shape, f32) * scale

    x_prompt = nrm(ks[0], (BATCH, SEQ, D_MODEL), 1.0)
    x_sample = nrm(ks[1], (DEC_BATCH, DEC_SEQ, D_MODEL), 1.0)
    state_gdn_conv = nrm(ks[2], (DEPTH, DEC_BATCH, CONV_W - 1, GDN_CONV_DIM), 1.0)
    state_gdn_S = nrm(ks[3], (DEPTH, DEC_BATCH, GDN_HEADS, GDN_DK, GDN_DV), GDN_DK ** -0.5)
    state_mlstm_C = nrm(ks[4], (DEPTH, DEC_BATCH, ML_HEADS, ML_DK, ML_DV), 0.1)
    state_mlstm_n = nrm(ks[5], (DEPTH, DEC_BATCH, ML_HEADS, ML_DK), 0.1)
    state_mlstm_m = jax.random.uniform(ks[6], (DEPTH, DEC_BATCH, ML_HEADS), f32, 0.0, 3.0)
    norm_mix = 1.0 + nrm(ks[7], (DEPTH, D_MODEL), 0.02)
    w_in = nrm(ks[8], (DEPTH, D_MODEL, IN_DIM), D_MODEL ** -0.5)
    gdn_conv_w = nrm(ks[9], (DEPTH, CONV_W, GDN_CONV_DIM), CONV_W ** -0.5)
    gdn_A_log = jnp.log(jax.random.uniform(ks[10], (DEPTH, GDN_HEADS), f32, 0.5, 4.0))
    dt0 = jnp.exp(jax.random.uniform(ks[11], (DEPTH, GDN_HEADS), f32, math.log(1e-3), math.log(1e-1)))
    gdn_dt_bias = dt0 + jnp.log(-jnp.expm1(-dt0))
    gdn_norm = 1.0 + nrm(ks[12], (DEPTH, GDN_DV), 0.02)
    ml_i_bias = nrm(ks[13], (DEPTH, ML_HEADS), 0.1) - 1.0
    ml_f_bias = jax.random.uniform(ks[14], (DEPTH, ML_HEADS), f32, 3.0, 6.0)
    ml_norm = 1.0 + nrm(ks[15], (DEPTH, ML_V), 0.02)
    w_branch_gdn = nrm(ks[16], (DEPTH, GDN_V, D_MODEL), GDN_V ** -0.5)
    w_branch_ml = nrm(ks[17], (DEPTH, ML_V, D_MODEL), ML_V ** -0.5)
    w_out = nrm(ks[18], (DEPTH, D_MODEL, D_MODEL), D_MODEL ** -0.5)
    norm_mlp = 1.0 + nrm(ks[19], (DEPTH, D_MODEL), 0.02)
    w_up = nrm(ks[20], (DEPTH, D_MODEL, D_FF), D_MODEL ** -0.5)
    w_down = nrm(ks[21], (DEPTH, D_FF, D_MODEL), D_FF ** -0.5)
    norm_final = 1.0 + nrm(ks[22], (D_MODEL,), 0.02)
    return {'x_prompt': x_prompt, 'x_sample': x_sample,
            'state_gdn_conv': state_gdn_conv, 'state_gdn_S': state_gdn_S,
            'state_mlstm_C': state_mlstm_C, 'state_mlstm_n': state_mlstm_n, 'state_mlstm_m': state_mlstm_m,
            'norm_mix': norm_mix, 'w_in': w_in, 'gdn_conv_w': gdn_conv_w, 'gdn_A_log': gdn_A_log,
            'gdn_dt_bias': gdn_dt_bias, 'gdn_norm': gdn_norm, 'ml_i_bias': ml_i_bias, 'ml_f_bias': ml_f_bias,
            'ml_norm': ml_norm, 'w_branch_gdn': w_branch_gdn, 'w_branch_ml': w_branch_ml, 'w_out': w_out,
            'norm_mlp': norm_mlp, 'w_up': w_up, 'w_down': w_down, 'norm_final': norm_final}


def reference(x_prompt, x_sample, state_gdn_conv, state_gdn_S, state_mlstm_C, state_mlstm_n, state_mlstm_m,
              norm_mix, w_in, gdn_conv_w, gdn_A_log, gdn_dt_bias, gdn_norm, ml_i_bias, ml_f_bias, ml_norm,
              w_branch_gdn, w_branch_ml, w_out, norm_mlp, w_up, w_down, norm_final):
    B = x_prompt.shape[0]
    dt = x_prompt.dtype
    z_conv = jnp.zeros((DEPTH, B, CONV_W - 1, GDN_CONV_DIM), dt)
    z_S = jnp.zeros((DEPTH, B, GDN_HEADS, GDN_DK, GDN_DV), dt)
    z_C = jnp.zeros((DEPTH, B, ML_HEADS, ML_DK, ML_DV), dt)
    z_n = jnp.zeros((DEPTH, B, ML_HEADS, ML_DK), dt)
    z_m = jnp.zeros((DEPTH, B, ML_HEADS), dt)
    y_prompt, conv_p, S_p, C_p, n_p, m_p = _trunk(
        x_prompt, z_conv, z_S, z_C, z_n, z_m, norm_mix, w_in, gdn_conv_w, gdn_A_log, gdn_dt_bias,
        gdn_norm, ml_i_bias, ml_f_bias, ml_norm, w_branch_gdn, w_branch_ml, w_out, norm_mlp,
        w_up, w_down, norm_final)
    y_sample, conv_s, S_s, C_s, n_s, m_s = _trunk(
        x_sample, state_gdn_conv, state_gdn_S, state_mlstm_C, state_mlstm_n, state_mlstm_m,
        norm_mix, w_in, gdn_conv_w, gdn_A_log, gdn_dt_bias, gdn_norm, ml_i_bias, ml_f_bias, ml_norm,
        w_branch_gdn, w_branch_ml, w_out, norm_mlp, w_up, w_down, norm_final)
    return (y_prompt, y_sample, conv_p, S_p, C_p, n_p, m_p, conv_s, S_s, C_s, n_s, m_s)
```

```python
import os
import numpy as np
import concourse.bass as bass
import concourse.mybir as mybir
from concourse.bass_utils import run_bass_kernel_spmd

F32 = mybir.dt.float32
BF16 = mybir.dt.bfloat16
AF = mybir.ActivationFunctionType
ALU = mybir.AluOpType
AX = mybir.AxisListType

NCORES = 8
D = 1024
KC = 8
SEQ = 2048
NSS = 16
TS_ = 4
T = SEQ + NSS * TS_
DEPTH = 2
IN_DIM = 5648
DFF = 4096
EPS = 1e-6
NEG = -30000.0
SEM_LIMIT = 8000
TILES = [(0, 512), (512, 512), (1024, 512), (1536, 512), (2048, 64)]


class Sched:
    ENGS = ("pe", "act", "dve", "pool", "sp")

    def __init__(self, nc):
        self.nc = nc
        self.ops = {e: [] for e in self.ENGS}
        self.csem = {}
        self.ccnt = {}
        self.nsem = 0
        for e in self.ENGS:
            self.csem[e] = self._newsem("c_" + e)
            self.ccnt[e] = 0
        self.dsem = {}
        self.dcnt = {}
        self.drr = {}
        for q in ("sp", "pool", "act"):
            self.dsem[q] = [self._newsem("d_%s%d" % (q, i)) for i in range(8)]
            self.dcnt[q] = [0] * 8
            self.drr[q] = 0
        self.last_w = {}
        self.readers = {}
        self.seen = {e: {} for e in self.ENGS}
        self.pending = {e: [] for e in self.ENGS}
        self.latest = {}

    def _newsem(self, name):
        self.nsem += 1
        return self.nc.alloc_semaphore("%s_%d" % (name, self.nsem))

    def op(self, eng, fn, r=(), w=(), dma=False):
        deps = {}

        def add(cid, kind):
            if cid is None:
                return
            s, v, peng = cid
            if eng == "pe" and peng == "pe":
                return
            if peng == eng and kind == "war" and not dma:
                return
            k = id(s)
            if k not in deps or deps[k][1] < v:
                deps[k] = (s, v)

        for t in r:
            add(self.last_w.get(t), "raw")
        for t in w:
            add(self.last_w.get(t), "waw")
            for c in self.readers.get(t, ()):
                add(c, "war")
        waits = list(self.pending[eng])
        self.pending[eng] = []
        for k, (s, v) in deps.items():
            if self.seen[eng].get(k, 0) < v:
                waits.append((s, v))
                self.seen[eng][k] = v
        if dma:
            q = eng
            i = self.drr[q]
            self.drr[q] = (i + 1) % len(self.dsem[q])
            if self.dcnt[q][i] + 16 > SEM_LIMIT:
                s_old = self.dsem[q][i]
                if self.seen[eng].get(id(s_old), 0) < self.dcnt[q][i]:
                    waits.append((s_old, self.dcnt[q][i]))
                    self.seen[eng][id(s_old)] = self.dcnt[q][i]
                self.dsem[q][i] = self._newsem("d_" + q)
                self.dcnt[q][i] = 0
            s = self.dsem[q][i]
            if self.dcnt[q][i] > 0 and self.seen[eng].get(id(s), 0) < self.dcnt[q][i]:
                waits.append((s, self.dcnt[q][i]))
                self.seen[eng][id(s)] = self.dcnt[q][i]
            self.dcnt[q][i] += 16
            cid = (s, self.dcnt[q][i], "dma")
            inc = (s, 16)
        else:
            if self.ccnt[eng] + 1 > SEM_LIMIT:
                self.csem[eng] = self._newsem("c_" + eng)
                self.ccnt[eng] = 0
            self.ccnt[eng] += 1
            s = self.csem[eng]
            cid = (s, self.ccnt[eng], eng)
            inc = (s, 1)
        self.latest[id(cid[0])] = (cid[0], cid[1])
        self.ops[eng].append((waits, fn, inc))
        for t in r:
            self.readers.setdefault(t, []).append(cid)
        for t in w:
            self.last_w[t] = cid
            self.readers[t] = []
        return cid

    def barrier(self):
        for e in self.ENGS:
            for k, (s, v) in self.latest.items():
                if self.seen[e].get(k, 0) < v:
                    self.pending[e].append((s, v))
                    self.seen[e][k] = v
        self.last_w = {}
        self.readers = {}

    def emit(self):
        nc = self.nc
        fin = []
        for k, (s, v) in self.latest.items():
            if self.seen["sp"].get(k, 0) < v:
                fin.append((s, v))
        fin = self.pending["sp"] + fin
        ops = self.ops

        def run(e, name, extra=()):
            for waits, fn, inc in ops[name]:
                for s, v in waits:
                    e.wait_ge(s, v)
                ins = fn(e)
                ins.then_inc(inc[0], inc[1])
            for s, v in extra:
                e.wait_ge(s, v)

        with nc.Block() as block:
            @block.tensor
            def _(e):
                run(e, "pe")

            @block.scalar
            def _(e):
                run(e, "act")

            @block.vector
            def _(e):
                run(e, "dve")

            @block.gpsimd
            def _(e):
                run(e, "pool")

            @block.sync
            def _(e):
                run(e, "sp", fin)


class Arena:
    def __init__(self, nc, base, top):
        self.nc = nc
        self.off = base
        self.top = top
        self.n = 0
        self.peak = base

    def alloc(self, shape, dtype, name="t"):
        nb = 1
        for s in shape[1:]:
            nb *= s
        nb *= 4 if dtype == F32 else 2
        off = (self.off + 63) // 64 * 64
        assert off + nb <= self.top, "SBUF arena overflow: %s %s need %d have %d" % (name, shape, nb, self.top - off)
        self.off = off + nb
        self.peak = max(self.peak, self.off)
        self.n += 1
        return self.nc.alloc_sbuf_tensor_at("%s_%d" % (name, self.n), list(shape), dtype, offset=off)

    def mark(self):
        return self.off

    def release(self, m):
        self.off = m


def MM(out, lhsT, rhs, start=True, stop=True):
    return lambda e: e.matmul(out, lhsT=lhsT, rhs=rhs, start=start, stop=stop)


def TR(out, in_, ident):
    return lambda e: e.transpose(out, in_, ident)


def TT(out, a, b, op):
    return lambda e: e.tensor_tensor(out=out, in0=a, in1=b, op=op)


def TSC(out, a, s1, op0, s2=None, op1=None):
    if op1 is None:
        return lambda e: e.tensor_scalar(out=out, in0=a, scalar1=s1, scalar2=None, op0=op0)
    return lambda e: e.tensor_scalar(out=out, in0=a, scalar1=s1, scalar2=s2, op0=op0, op1=op1)


def STT(out, a, s, b, op0, op1):
    return lambda e: e.scalar_tensor_tensor(out=out, in0=a, scalar=s, in1=b, op0=op0, op1=op1)


def ACT(out, in_, func, bias=None, scale=None, accum=None):
    kw = {}
    if bias is not None:
        kw["bias"] = bias
    if scale is not None:
        kw["scale"] = scale
    if accum is not None:
        kw["accum_out"] = accum
    return lambda e: e.activation(out=out, in_=in_, func=func, **kw)


def CP(out, in_):
    return lambda e: e.tensor_copy(out=out, in_=in_)


def MSET(ap, v):
    return lambda e: e.memset(ap, v)


def RED(out, in_, op, axis=AX.X):
    return lambda e: e.tensor_reduce(out=out, in_=in_, axis=axis, op=op)


def RCP(out, in_):
    return lambda e: e.reciprocal(out=out, in_=in_)


def DMA(out, in_):
    return lambda e: e.dma_start(out=out, in_=in_)


def bc(ap, shape, axis):
    return ap.unsqueeze(axis).to_broadcast(list(shape))


def _seq_of(mode):
    if mode == "p":
        return np.zeros(64, np.int64)
    return np.arange(64) // TS_


def build_consts():
    cols = {}
    parts = []
    off = [0]

    def put(name, arr):
        arr = np.asarray(arr, np.float32)
        a = np.zeros((128, arr.shape[1]), np.float32)
        a[: arr.shape[0]] = arr
        cols[name] = (off[0], arr.shape[1])
        parts.append(a)
        off[0] += arr.shape[1]

    put("ident", np.eye(128))
    L2 = np.zeros((128, 64), np.float32)
    L2[:64] = 1.0
    L2[64:] = np.eye(64)
    put("L2", L2)
    put("ones", np.ones((128, 128)))
    maskB = {}
    for mode in ("p", "s"):
        sq = _seq_of(mode)
        k = np.arange(64)
        same = sq[:, None] == sq[None, :]
        last = np.array([np.max(np.nonzero(sq == sq[t])[0]) for t in range(64)])
        cum = ((k[:, None] <= k[None, :]) & same).astype(np.float32)
        tot = same.astype(np.float32)
        lastsel = (k[:, None] == last[None, :]).astype(np.float32)
        put("CUM" + mode, cum)
        put("TOT" + mode, tot)
        put("LASTSEL" + mode, lastsel)
        ns = 1 if mode == "p" else NSS
        seqoh = (sq[:, None] == np.arange(ns)[None, :]).astype(np.float32)
        lastoh = seqoh * (k == last)[:, None]
        put("SEQOH" + mode, seqoh)
        put("LASTOH" + mode, lastoh)
        lo_s = np.where((k[:, None] > k[None, :]) & same, 0.0, NEG)
        lo_i = np.where((k[:, None] >= k[None, :]) & same, 0.0, NEG)
        up_s = np.where((k[None, :] > k[:, None]) & same, 0.0, NEG)
        up_i = np.where((k[None, :] >= k[:, None]) & same, 0.0, NEG)
        rep = lambda m: np.repeat(m[:, None, :], 4, axis=1).reshape(64, 256)
        maskB["G" + mode] = np.concatenate([rep(lo_s), rep(up_s), rep(up_i)], axis=1)
        maskB["M1" + mode] = rep(lo_i)
        maskB["M2" + mode] = rep(up_i)
    sq = _seq_of("s")
    smf = (sq[None, :] == np.arange(NSS)[:, None]).astype(np.float32).reshape(1, NSS * 64)
    cC = np.repeat(smf, 128, axis=0).astype(np.float32)
    put("EXPAND", (np.arange(NSS)[:, None] == sq[None, :]).astype(np.float32))
    cA = np.concatenate(parts, axis=1)
    names = ["Gp", "Gs", "M1p", "M1s", "M2p", "M2s"]
    bcols = {}
    o = 0
    bl = []
    for n in names:
        bcols[n] = (o, maskB[n].shape[1])
        o += maskB[n].shape[1]
        bl.append(maskB[n])
    cB = np.concatenate(bl, axis=1).astype(np.float32)
    return cA, cB, cC, cols, bcols


_CA, _CB, _CC, _CCOL, _BCOL = build_consts()


OFF_QG, OFF_KG, OFF_VG, OFF_ZG = 0, 512, 1024, 1536
OFF_QM, OFF_KM, OFF_VM, OFF_OM = 2056, 2312, 2568, 3080
OFF_GG, OFF_GM = 3600, 4624

PV_NMIX, PV_NMLP, PV_CONVW, PV_GNORM, PV_MLNORM, PV_GBIAS, PV_GSIGN, PV_ALOG = 0, 8, 16, 64, 65, 69, 85, 101
PV_L = 105
PV_NFIN = 2 * PV_L
PV_COLS = PV_NFIN + 8


class B:
    pass


DEBUG_DUMP = bool(int(os.environ.get("MK_DEBUG", "0")))


def build_program(stage=99):
    nc = bass.Bass("TRN2", target_bir_lowering=False)
    S = Sched(nc)
    g = B()
    g.nc, g.S = nc, S

    def din(name, shape, dt=F32):
        return nc.dram_tensor(name, list(shape), dt, kind="ExternalInput").ap()

    def dout(name, shape, dt=F32):
        return nc.dram_tensor(name, list(shape), dt, kind="ExternalOutput").ap()

    g.xin = din("xin", [T, D])
    g.w_in = din("w_in", [DEPTH, D, IN_DIM])
    g.wg = din("wg", [DEPTH, D, 16])
    g.w_bg = din("w_bg", [DEPTH, 512, D])
    g.w_bm = din("w_bm", [DEPTH, 512, D])
    g.w_out = din("w_out", [DEPTH, D, D])
    g.w_up = din("w_up", [DEPTH, D, DFF])
    g.w_down = din("w_down", [DEPTH, DFF, D])
    g.pv = din("pv", [128, PV_COLS])
    g.cA = din("cA", list(_CA.shape))
    g.cB = din("cB", list(_CB.shape))
    g.cC = din("cC", list(_CC.shape))
    g.conv0 = din("conv0", [DEPTH, NSS, 3, 1536])
    g.S0 = din("S0", [DEPTH, NSS, 4, 128, 128])
    g.C0 = din("C0", [DEPTH, NSS, 4, 64, 128])
    g.n0 = din("n0", [DEPTH, NSS, 4, 64])
    g.m0 = din("m0", [DEPTH, NSS, 4])
    g.y = dout("y", [T, D])
    g.convp = dout("convp", [DEPTH, 3, 1536])
    g.Sp = dout("Sp", [DEPTH, 4, 128, 128])
    g.Cp = dout("Cp", [DEPTH, 4, 64, 128])
    g.np_ = dout("np_", [DEPTH, 4, 64])
    g.mp = dout("mp", [DEPTH, 4])
    g.convs = dout("convs", [DEPTH, NSS, 3, 1536])
    g.Ss = dout("Ss", [DEPTH, NSS, 4, 128, 128])
    g.Cs = dout("Cs", [DEPTH, NSS, 4, 64, 128])
    g.ns = dout("ns", [DEPTH, NSS, 4, 64])
    g.ms = dout("ms", [DEPTH, NSS, 4])

    A = Arena(nc, 16640, 229344)
    g.A = A
    g.banks = [nc.alloc_psum_tensor("psum%d" % i, [128, 512], F32) for i in range(6)]
    g.psb6 = nc.alloc_psum_tensor("psum6b", [128, 1024], BF16)
    g.banks.append(None)
    g.banks.append(nc.alloc_psum_tensor("psum7", [128, 512], F32))

    def bank(b):
        return g.banks[b]
    g.bank = bank

    g.xT = A.alloc([128, KC, T], F32, "xT")
    g.mixT = A.alloc([128, KC, T], BF16, "mixT")
    g.cAt = A.alloc([128, _CA.shape[1]], F32, "cA")
    g.pvt = A.alloc([128, PV_COLS], F32, "pv")
    g.identb = A.alloc([128, 128], BF16, "identb")
    g.onesb = A.alloc([128, 128], BF16, "onesb")
    g.smf = A.alloc([128, NSS, 64], BF16, "smf")
    g.epsb = A.alloc([128, 1], F32, "epsb")
    g.oneb = A.alloc([128, 1], F32, "oneb")

    def cst(name, rows=128):
        o, n = _CCOL[name]
        return g.cAt[0:rows, o:o + n]
    g.cst = cst
    g.ident = cst("ident")

    S.op("sp", DMA(g.cAt[:], g.cA), w=["cA"], dma=True)
    S.op("sp", DMA(g.pvt[:], g.pv), w=["pv"], dma=True)
    S.op("pool", DMA(g.smf[:].rearrange("p s f -> p (s f)"), g.cC), w=["smf"], dma=True)
    S.op("dve", CP(g.identb[:], g.ident), r=["cA"], w=["identb"])
    S.op("dve", MSET(g.onesb[:], 1.0), w=["onesb"])
    S.op("dve", MSET(g.epsb[:], EPS), w=["epsb"])
    S.op("dve", MSET(g.oneb[:], 1.0), w=["oneb"])
    S.barrier()

    pass0_load(g)
    for l in range(DEPTH):
        if stage >= 2:
            pass_gdn(g, l)
        if stage >= 3:
            pass_mlstm(g, l)
        if DEBUG_DUMP and l == 0:
            dbg = nc.dram_tensor("dbg_mix", [128, KC, T], BF16, kind="ExternalOutput").ap()
            S.op("sp", DMA(dbg, g.mixT[:]), dma=True)
            S.barrier()
        if stage >= 1:
            pass_mixout(g, l, with_mixer=(stage >= 2))
            pass_mlp(g, l)
    pass_final(g)
    S.emit()
    g.peak = A.peak
    return nc


def wview(w2d, kc):
    return w2d.rearrange("(c p) n -> p c n", p=128)


def pass0_load(g):
    S, A = g.S, g.A
    m = A.mark()
    xt = [A.alloc([128, D], F32, "xtok") for _ in range(2)]
    ntt = (T + 127) // 128
    for i in range(ntt):
        t0 = i * 128
        n = min(128, T - t0)
        buf = xt[i % 2]
        S.op("sp", DMA(buf[0:n, :], g.xin[t0:t0 + n, :]), w=[("xtok", i % 2)], dma=True)
        for c in range(KC):
            bk = c % 2
            S.op("pe", TR(g.bank(bk)[:, 0:n], buf[0:n, c * 128:(c + 1) * 128], g.ident[0:n, 0:n]),
                 r=[("xtok", i % 2)], w=[("ps", bk)])
            eng = "act" if c % 2 == 0 else "dve"
            if eng == "act":
                S.op("act", ACT(g.xT[:, c, t0:t0 + n], g.bank(bk)[:, 0:n], AF.Copy), r=[("ps", bk)], w=[("xT0", c)])
            else:
                S.op("dve", CP(g.xT[:, c, t0:t0 + n], g.bank(bk)[:, 0:n]), r=[("ps", bk)], w=[("xT0", c)])
    S.barrier()
    A.release(m)


def rmsnorm_tile(g, hT, hoff, t0, n, wcol, sqbufs, rstd, psb, tag, sqtok=None):
    S = g.S
    pb = g.bank(psb)
    btag = tag[0] if isinstance(tag, tuple) else tag
    if sqtok is None:
        sqtok = lambda i: ("sq", btag, i)
    for c in range(KC):
        sq = sqbufs[c % 2]
        S.op("act", ACT(sq[:, 0:n], g.xT[:, c, t0:t0 + n], AF.Square), r=["xT"], w=[sqtok(c % 2)])
        S.op("pe", MM(pb[:, 0:n], g.onesb[:, :], sq[:, 0:n], start=(c == 0), stop=(c == KC - 1)),
             r=[sqtok(c % 2)], w=[("ps", psb)])
    S.op("act", ACT(rstd[:, 0:n], pb[:, 0:n], AF.Ln, bias=g.epsb[:, 0:1], scale=1.0 / D), r=[("ps", psb)], w=[("rstd", btag)])
    S.op("act", ACT(rstd[:, 0:n], rstd[:, 0:n], AF.Exp, scale=-0.5), r=[("rstd", btag)], w=[("rstd", btag)])
    for c in range(KC):
        S.op("dve", STT(hT[:, c, hoff:hoff + n], g.xT[:, c, t0:t0 + n], g.pvt[:, wcol + c:wcol + c + 1], rstd[:, 0:n],
                        ALU.mult, ALU.mult), r=["xT", ("rstd", btag)], w=[("hT", tag)])


def wload(g, dst_ap, src_ap, tok):
    g.S.op("pool", DMA(dst_ap, src_ap), w=[tok], dma=True)


def pass_mlp(g, l):
    S, A = g.S, g.A
    m = A.mark()
    hT = A.alloc([128, KC, T], BF16, "hT")
    aT = g.mixT
    sqb = [A.alloc([128, 512], BF16, "sq") for _ in range(2)]
    rstd = A.alloc([128, 512], F32, "rstd")
    Wu = [A.alloc([128, KC, 512], BF16, "Wu") for _ in range(2)]
    Wd = [A.alloc([128, KC, 512], BF16, "Wd") for _ in range(2)]
    r32 = [A.alloc([128, 512], F32, "r32") for _ in range(2)]
    wup = wview(g.w_up[l], KC)
    nu = 0
    nd = 0
    for ti, (t0, n) in enumerate(TILES):
        rmsnorm_tile(g, hT, t0, t0, n, PV_L * l + PV_NMLP, sqb, rstd, 7, ("mlp", ti))
    bk = 0
    for grp in range(4):
        for half in range(2):
            slot = nu % 2
            nu += 1
            c0 = grp * 1024 + half * 512
            wload(g, Wu[slot][:], wup[:, :, c0:c0 + 512], ("Wu", slot))
            for j in range(4):
                jj = half * 4 + j
                for ti, (t0, n) in enumerate(TILES):
                    b = bk % 2
                    bk += 1
                    for k in range(KC):
                        S.op("pe", MM(g.bank(b)[:, 0:n], Wu[slot][:, k, j * 128:(j + 1) * 128], hT[:, k, t0:t0 + n],
                                      start=(k == 0), stop=(k == KC - 1)),
                             r=[("Wu", slot), ("hT", ("mlp", ti))], w=[("ps", b)])
                    S.op("act", ACT(r32[b][:, 0:n], g.bank(b)[:, 0:n], AF.Relu), r=[("ps", b)], w=[("r32", b)])
                    S.op("pool", TT(aT[:, jj, t0:t0 + n], r32[b][:, 0:n], r32[b][:, 0:n], ALU.mult),
                         r=[("r32", b)], w=[("aT", jj, ti)])
        wdn = g.w_down[l][grp * 1024:(grp + 1) * 1024, :].rearrange("(c p) n -> p c n", p=128)
        for half in range(2):
            slot = nd % 2
            nd += 1
            wload(g, Wd[slot][:], wdn[:, :, half * 512:(half + 1) * 512], ("Wd", slot))
            for j in range(4):
                mm_ = half * 4 + j
                for ti, (t0, n) in enumerate(TILES):
                    b = bk % 2
                    bk += 1
                    for k in range(KC):
                        S.op("pe", MM(g.bank(b)[:, 0:n], Wd[slot][:, k, j * 128:(j + 1) * 128], aT[:, k, t0:t0 + n],
                                      start=(k == 0), stop=(k == KC - 1)),
                             r=[("Wd", slot), ("aT", k, ti)], w=[("ps", b)])
                    S.op("dve", TT(g.xT[:, mm_, t0:t0 + n], g.xT[:, mm_, t0:t0 + n], g.bank(b)[:, 0:n], ALU.add),
                         r=[("ps", b), "xT"], w=["xT"])
    S.barrier()
    A.release(m)


def pass_mixout(g, l, with_mixer=True):
    S, A = g.S, g.A
    m = A.mark()
    hT = A.alloc([128, KC, T], BF16, "hT")
    HALF = [[0, 1], [2, 3, 4]]
    mg = A.alloc([128, KC, 1088], BF16, "merged")
    sqb = [A.alloc([128, 512], BF16, "sq") for _ in range(2)]
    rstd = A.alloc([128, 512], F32, "rstd")
    W8 = [A.alloc([128, KC, 256], BF16, "W8") for _ in range(4)]
    W4 = [A.alloc([128, 4, 256], BF16, "W4") for _ in range(4)]
    tmp = [A.alloc([128, 512], F32, "tmp") for _ in range(4)]
    tb = [A.alloc([128, 512], BF16, "tb") for _ in range(2)]
    win = wview(g.w_in[l], KC)
    for ti, (t0, n) in enumerate(TILES):
        rmsnorm_tile(g, hT, t0, t0, n, PV_L * l + PV_NMIX, sqb, rstd, 7, ("mix", ti))
    n8 = [0]
    n4 = [0]

    def slot8():
        s = n8[0] % 4
        n8[0] += 1
        return s

    def slot4():
        s = n4[0] % 4
        n4[0] += 1
        return s
    bk = 0
    if with_mixer:
        for which, off, func in ((0, OFF_ZG, AF.Silu), (1, OFF_OM, AF.Sigmoid)):
            for pair in range(2):
                s8 = slot8()
                wload(g, W8[s8][:], win[:, :, off + pair * 256: off + pair * 256 + 256], ("W8", s8))
                for j in range(2):
                    c = which * 4 + pair * 2 + j
                    for ti, (t0, n) in enumerate(TILES):
                        b = bk % 2
                        bk += 1
                        for k in range(KC):
                            S.op("pe", MM(g.bank(b)[:, 0:n], W8[s8][:, k, j * 128:(j + 1) * 128], hT[:, k, t0:t0 + n],
                                          start=(k == 0), stop=(k == KC - 1)),
                                 r=[("W8", s8), ("hT", ("mix", ti))], w=[("ps", b)])
                        S.op("act", ACT(tb[b][:, 0:n], g.bank(b)[:, 0:n], func), r=[("ps", b)], w=[("tb", b)])
                        S.op("pool", TT(g.mixT[:, c, t0:t0 + n], g.mixT[:, c, t0:t0 + n], tb[b][:, 0:n], ALU.mult),
                             r=[("tb", b), ("mixT", c, ti)], w=[("mixT", c, ti)])
    wbg = g.w_bg[l].rearrange("(c p) n -> p c n", p=128)
    wbm = g.w_bm[l].rearrange("(c p) n -> p c n", p=128)
    wout = wview(g.w_out[l], KC)
    for hf, tis in enumerate(HALF):
        hbase = TILES[tis[0]][0]
        if with_mixer:
            for pair in range(4):
                sg8, sm8, sg4, sm4 = slot8(), slot8(), slot4(), slot4()
                c0 = pair * 256
                wload(g, W4[sg4][:], wbg[:, :, c0:c0 + 256], ("W4", sg4))
                wload(g, W8[sg8][:], win[:, :, OFF_GG + c0:OFF_GG + c0 + 256], ("W8", sg8))
                wload(g, W4[sm4][:], wbm[:, :, c0:c0 + 256], ("W4", sm4))
                wload(g, W8[sm8][:], win[:, :, OFF_GM + c0:OFF_GM + c0 + 256], ("W8", sm8))
                for j in range(2):
                    nn = pair * 2 + j
                    for ti in tis:
                        t0, n = TILES[ti]
                        for k in range(4):
                            S.op("pe", MM(g.bank(0)[:, 0:n], W4[sg4][:, k, j * 128:(j + 1) * 128], g.mixT[:, k, t0:t0 + n],
                                          start=(k == 0), stop=(k == 3)),
                                 r=[("W4", sg4), ("mixT", k, ti)], w=[("ps", 0)])
                        for k in range(KC):
                            S.op("pe", MM(g.bank(1)[:, 0:n], W8[sg8][:, k, j * 128:(j + 1) * 128], hT[:, k, t0:t0 + n],
                                          start=(k == 0), stop=(k == KC - 1)),
                                 r=[("W8", sg8), ("hT", ("mix", ti))], w=[("ps", 1)])
                        for k in range(4):
                            S.op("pe", MM(g.bank(2)[:, 0:n], W4[sm4][:, k, j * 128:(j + 1) * 128], g.mixT[:, 4 + k, t0:t0 + n],
                                          start=(k == 0), stop=(k == 3)),
                                 r=[("W4", sm4), ("mixT", 4 + k, ti)], w=[("ps", 2)])
                        for k in range(KC):
                            S.op("pe", MM(g.bank(3)[:, 0:n], W8[sm8][:, k, j * 128:(j + 1) * 128], hT[:, k, t0:t0 + n],
                                          start=(k == 0), stop=(k == KC - 1)),
                                 r=[("W8", sm8), ("hT", ("mix", ti))], w=[("ps", 3)])
                        S.op("act", ACT(tmp[0][:, 0:n], g.bank(1)[:, 0:n], AF.Sigmoid), r=[("ps", 1)], w=[("tmp", 0)])
                        S.op("act", ACT(tmp[1][:, 0:n], g.bank(3)[:, 0:n], AF.Sigmoid), r=[("ps", 3)], w=[("tmp", 1)])
                        S.op("dve", TT(tmp[2][:, 0:n], tmp[0][:, 0:n], g.bank(0)[:, 0:n], ALU.mult),
                             r=[("tmp", 0), ("ps", 0)], w=[("tmp", 2)])
                        S.op("dve", TT(tmp[3][:, 0:n], tmp[1][:, 0:n], g.bank(2)[:, 0:n], ALU.mult),
                             r=[("tmp", 1), ("ps", 2)], w=[("tmp", 3)])
                        S.op("pool", TT(mg[:, nn, t0 - hbase:t0 - hbase + n], tmp[2][:, 0:n], tmp[3][:, 0:n], ALU.add),
                             r=[("tmp", 2), ("tmp", 3)], w=[("mg", nn, ti)])
            for pair in range(4):
                s8 = slot8()
                wload(g, W8[s8][:], wout[:, :, pair * 256:(pair + 1) * 256], ("W8", s8))
                for j in range(2):
                    mm_ = pair * 2 + j
                    for ti in tis:
                        t0, n = TILES[ti]
                        b = 4 + (bk % 2)
                        bk += 1
                        for k in range(KC):
                            S.op("pe", MM(g.bank(b)[:, 0:n], W8[s8][:, k, j * 128:(j + 1) * 128],
                                          mg[:, k, t0 - hbase:t0 - hbase + n], start=(k == 0), stop=(k == KC - 1)),
                                 r=[("W8", s8), ("mg", k, ti)], w=[("ps", b)])
                        S.op("dve", TT(g.xT[:, mm_, t0:t0 + n], g.xT[:, mm_, t0:t0 + n], g.bank(b)[:, 0:n], ALU.add),
                             r=[("ps", b), "xT"], w=["xT"])
    S.barrier()
    A.release(m)


def pass_final(g):
    S, A = g.S, g.A
    m = A.mark()
    sqb = [A.alloc([128, 512], BF16, "sq") for _ in range(2)]
    rstd = A.alloc([128, 512], F32, "rstd")
    yT = A.alloc([128, KC, 512], F32, "yT")
    yt = [A.alloc([128, D], F32, "ytok") for _ in range(2)]
    nst = 0
    for ti, (t0, n) in enumerate(TILES):
        pb = g.bank(7)
        for c in range(KC):
            sq = sqb[c % 2]
            S.op("act", ACT(sq[:, 0:n], g.xT[:, c, t0:t0 + n], AF.Square), r=["xT"], w=[("sq", c % 2)])
            S.op("pe", MM(pb[:, 0:n], g.onesb[:, :], sq[:, 0:n], start=(c == 0), stop=(c == KC - 1)),
                 r=[("sq", c % 2)], w=[("ps", 7)])
        S.op("act", ACT(rstd[:, 0:n], pb[:, 0:n], AF.Ln, bias=g.epsb[:, 0:1], scale=1.0 / D), r=[("ps", 7)], w=["rstd"])
        S.op("act", ACT(rstd[:, 0:n], rstd[:, 0:n], AF.Exp, scale=-0.5), r=["rstd"], w=["rstd"])
        for c in range(KC):
            S.op("dve", STT(yT[:, c, 0:n], g.xT[:, c, t0:t0 + n], g.pvt[:, PV_NFIN + c:PV_NFIN + c + 1], rstd[:, 0:n],
                            ALU.mult, ALU.mult), r=["xT", "rstd"], w=["yT"])
        for s0 in range(0, n, 128):
            ns_ = min(128, n - s0)
            buf = yt[nst % 2]
            tok = ("ytok", nst % 2)
            nst += 1
            for c in range(KC):
                b = c % 2
                S.op("pe", TR(g.bank(b)[0:ns_, 0:128], yT[:, c, s0:s0 + ns_], g.ident[:, :]), r=["yT"], w=[("ps", b)])
                if c % 2 == 0:
                    S.op("act", ACT(buf[0:ns_, c * 128:(c + 1) * 128], g.bank(b)[0:ns_, 0:128], AF.Copy),
                         r=[("ps", b)], w=[tok])
                else:
                    S.op("dve", CP(buf[0:ns_, c * 128:(c + 1) * 128], g.bank(b)[0:ns_, 0:128]), r=[("ps", b)], w=[tok])
            S.op("sp", DMA(g.y[t0 + s0:t0 + s0 + ns_, :], buf[0:ns_, :]), r=[tok], dma=True)
    S.barrier()
    A.release(m)


def _fm(v, nch):
    return np.ascontiguousarray(np.asarray(v, np.float32).reshape(nch, 128).T)


def _build_pv(inp):
    pv = np.zeros((128, PV_COLS), np.float32)
    for l in range(DEPTH):
        o = PV_L * l
        pv[:, o + PV_NMIX:o + PV_NMIX + 8] = _fm(inp["norm_mix"][l], 8)
        pv[:, o + PV_NMLP:o + PV_NMLP + 8] = _fm(inp["norm_mlp"][l], 8)
        cw = np.asarray(inp["gdn_conv_w"][l], np.float32)
        pv[:, o + PV_CONVW:o + PV_CONVW + 48] = cw.reshape(4, 12, 128).transpose(2, 1, 0).reshape(128, 48)
        pv[:, o + PV_GNORM] = np.asarray(inp["gdn_norm"][l], np.float32)
        pv[:, o + PV_MLNORM:o + PV_MLNORM + 4] = _fm(inp["ml_norm"][l], 4)
        gb = np.zeros(16, np.float32)
        gb[4:8] = inp["gdn_dt_bias"][l]
        gb[8:12] = inp["ml_i_bias"][l]
        gb[12:16] = inp["ml_f_bias"][l]
        pv[:, o + PV_GBIAS:o + PV_GBIAS + 16] = gb[None, :]
        pv[:, o + PV_GSIGN:o + PV_GSIGN + 16] = np.array([-1] * 4 + [1] * 4 + [1] * 4 + [-1] * 4, np.float32)[None, :]
        pv[:, o + PV_ALOG:o + PV_ALOG + 4] = np.asarray(inp["gdn_A_log"][l], np.float32)[None, :]
    pv[:, PV_NFIN:PV_NFIN + 8] = _fm(inp["norm_final"], 8)
    return pv


_PROG = {}


def _get_prog(stage):
    if stage not in _PROG:
        _PROG[stage] = build_program(stage)
    return _PROG[stage]


def kernel(**inputs):
    stage = int(os.environ.get("MK_STAGE", "99"))
    inp = {k: np.asarray(v) for k, v in inputs.items()}
    f32 = lambda a: np.ascontiguousarray(a, dtype=np.float32)
    w_in = f32(inp["w_in"])
    wg = np.ascontiguousarray(np.concatenate([w_in[:, :, 2048:2056], w_in[:, :, 3592:3600]], axis=2))
    shared = {
        "w_in": w_in, "wg": wg, "w_bg": f32(inp["w_branch_gdn"]), "w_bm": f32(inp["w_branch_ml"]),
        "w_out": f32(inp["w_out"]), "w_up": f32(inp["w_up"]), "w_down": f32(inp["w_down"]),
        "pv": _build_pv(inp), "cA": _CA, "cB": _CB, "cC": _CC,
    }
    xp, xs = f32(inp["x_prompt"]), f32(inp["x_sample"])
    in_maps = []
    for c in range(NCORES):
        sl = slice(c * NSS, (c + 1) * NSS)
        m = dict(shared)
        m["xin"] = np.ascontiguousarray(np.concatenate([xp[c], xs[sl].reshape(NSS * TS_, D)], axis=0))
        m["conv0"] = f32(inp["state_gdn_conv"][:, sl])
        m["S0"] = f32(inp["state_gdn_S"][:, sl])
        m["C0"] = f32(inp["state_mlstm_C"][:, sl])
        m["n0"] = f32(inp["state_mlstm_n"][:, sl])
        m["m0"] = f32(inp["state_mlstm_m"][:, sl])
        in_maps.append(m)
    nc = _get_prog(stage)
    res = run_bass_kernel_spmd(nc, in_maps, core_ids=list(range(NCORES)))
    R = res.results
    y = np.stack([r["y"] for r in R])
    y_prompt = np.ascontiguousarray(y[:, :SEQ])
    y_sample = np.ascontiguousarray(y[:, SEQ:].reshape(NCORES * NSS, TS_, D))
    cat1 = lambda k: np.ascontiguousarray(np.stack([r[k] for r in R], axis=1))
    cats = lambda k: np.ascontiguousarray(np.concatenate([r[k] for r in R], axis=1))
    return (y_prompt, y_sample, cat1("convp"), cat1("Sp"), cat1("Cp"), cat1("np_"), cat1("mp"),
            cats("convs"), cats("Ss"), cats("Cs"), cats("ns"), cats("ms"))


NDT = F32
NL_P, NL_S = 5, 1
LN_QSCALE = float(np.log(128.0 ** -0.5))


def pass_gdn(g, l):
    S, A = g.S, g.A
    m = A.mark()
    pvo = PV_L * l
    cst = g.cst
    ident64 = cst("ident", 64)[:, 0:64]
    L2 = cst("L2")
    ones64 = cst("ones", 64)
    reg0 = A.mark()
    Wq = A.alloc([128, KC, 1536], BF16, "Wq")
    Wg = A.alloc([128, KC, 16], BF16, "Wg")
    win = wview(g.w_in[l], KC)
    for i in range(3):
        wload(g, Wq[:, :, i * 512:(i + 1) * 512], win[:, :, i * 512:(i + 1) * 512], ("Wq", i))
    wload(g, Wg[:], wview(g.wg[l], KC), "Wg")
    cdiag = A.alloc([128, 48, 128], BF16, "cdiag")
    for cj in range(48):
        S.op("pool", TSC(cdiag[:, cj, :], g.identb[:, :], g.pvt[:, pvo + PV_CONVW + cj:pvo + PV_CONVW + cj + 1], ALU.mult),
             w=[("cdiag", cj)])
    negA = A.alloc([64, 4], F32, "negA")
    S.op("act", ACT(negA[:], g.pvt[0:64, pvo + PV_ALOG:pvo + PV_ALOG + 4], AF.Exp), w=["negA"])
    S.op("dve", TSC(negA[:], negA[:], -1.0, ALU.mult), r=["negA"], w=["negA"])
    hT = A.alloc([128, KC, 512], BF16, "hT")
    rstd = A.alloc([128, 512], F32, "rstd")
    pcc = [A.alloc([128, 3 + 512], BF16, "pcc") for _ in range(2)]
    sqb = pcc
    reg1 = A.mark()
    hist = A.alloc([128, 12, 3], BF16, "hist")
    cst32 = A.alloc([128, 12, 48], F32, "cst32")
    qkvT = A.alloc([128, 12, 512], BF16, "qkvT")
    sqk = A.alloc([128, 8, 64], BF16, "sqk")
    cio = A.alloc([48, 768], F32, "cio")
    hists = A.alloc([128, 12, 48], BF16, "hists")
    G1 = A.alloc([64, 8, 8], F32, "G1")
    Lg = A.alloc([64, 8, 8], F32, "Lg")
    gv = A.alloc([64, 8, 4], F32, "gv")
    gcs = A.alloc([64, 8, 4], F32, "gcs")
    gtots = A.alloc([64, 8, 4], F32, "gtots")
    LN = A.alloc([64, 8, 8], F32, "LN")
    VEC = A.alloc([64, 8, 5, 4], F32, "VEC")
    tmpv = A.alloc([64, 8, 4], F32, "tmpv")
    SC = A.alloc([64, 8, 4, 4], F32, "SC")
    BV = A.alloc([64, 8, 3, 4], F32, "BV")
    rdl = A.alloc([64, 64], F32, "rdl")
    dl = A.alloc([128, 64], F32, "dl")
    RH = A.alloc([128, 768], F32, "RH")
    X = A.alloc([64, 768], F32, "X")
    DCQ = [A.alloc([64, 3, 4, 64], NDT, "DCQ") for _ in range(2)]
    TTb = A.alloc([64, 4, 64], BF16, "TTb")
    QKd = A.alloc([64, 4, 64], BF16, "QKd")
    kvtm = A.alloc([64, 8, 128], BF16, "kvtm")
    kbg = A.alloc([64, 4, 128], BF16, "kbg")
    vb = A.alloc([64, 4, 128], BF16, "vb")
    kd = A.alloc([64, 4, 128], BF16, "kd")
    wTs = A.alloc([128, 4, 64], BF16, "wTs")
    us = A.alloc([64, 4, 128], F32, "us")
    vn = A.alloc([64, 4, 128], BF16, "vn")
    o32 = A.alloc([64, 4, 128], F32, "o32")
    on = A.alloc([64, 4, 128], BF16, "on")
    sso = A.alloc([64, 4], F32, "sso")
    S32 = A.alloc([128, 4, 128], F32, "S32")
    Sb = A.alloc([128, 4, 128], BF16, "Sb")
    S.op("dve", MSET(S32[:], 0.0), w=["S32"])
    S.op("dve", MSET(Sb[:], 0.0), w=["Sb"])
    S.op("pool", MSET(hist[:], 0.0), w=["hist"])
    b6 = g.psb6

    for ti, (t0, n) in enumerate(TILES):
        sample = (ti == 4)
        nch = n // 64
        md = "s" if sample else "p"
        NS = NSS if sample else 1
        NL = NL_S if sample else NL_P
        rhm = "RH_m"
        if ti == 0 or sample:
            o_, n_ = _BCOL["G" + md]
            S.op("sp", DMA(RH[64:128, :], g.cB[:, o_:o_ + n_]), w=["RH_m"], dma=True)
        rmsnorm_tile(g, hT, 0, t0, n, pvo + PV_NMIX, sqb, rstd, 7, "g", sqtok=lambda i: ("pcc", i))
        for ch in range(nch):
            for k in range(KC):
                S.op("pe", MM(g.bank(2)[0:64, ch * 8:(ch + 1) * 8], hT[:, k, ch * 64:(ch + 1) * 64], Wg[:, k, 0:8],
                              start=(k == 0), stop=(k == KC - 1)), r=[("hT", "g"), "Wg"], w=[("ps", 2)])
        gps = g.bank(2)[0:64, 0:nch * 8].rearrange("p (c x) -> p c x", x=8)
        S.op("dve", TT(G1[:, 0:nch, :], gps, bc(g.pvt[0:64, pvo + PV_GBIAS:pvo + PV_GBIAS + 8], [64, nch, 8], 1), ALU.add),
             r=[("ps", 2)], w=["G1"])
        S.op("dve", TT(G1[:, 0:nch, :], G1[:, 0:nch, :], bc(g.pvt[0:64, pvo + PV_GSIGN:pvo + PV_GSIGN + 8], [64, nch, 8], 1),
                       ALU.mult), r=["G1"], w=["G1"])
        S.op("act", ACT(Lg[:, 0:nch, :], G1[:, 0:nch, :], AF.Exp), r=["G1"], w=["Lg"])
        S.op("act", ACT(Lg[:, 0:nch, :], Lg[:, 0:nch, :], AF.Ln, bias=g.oneb[0:64, 0:1]), r=["Lg"], w=["Lg"])
        S.op("dve", TSC(VEC[:, 0:nch, 3, :], Lg[:, 0:nch, 0:4], -1.0, ALU.mult), r=["Lg"], w=[("VEC", 3)])
        S.op("dve", TT(gv[:, 0:nch, :], Lg[:, 0:nch, 4:8], bc(negA[:, :], [64, nch, 4], 1), ALU.mult),
             r=["Lg", "negA"], w=["gv"])
        gvf = gv[:, 0:nch, :]
        S.op("pe", MM(g.bank(2)[0:64, 128:128 + nch * 4], cst("CUM" + md, 64), gvf), r=["gv"], w=[("ps", 2)])
        S.op("pe", MM(g.bank(2)[0:64, 192:192 + nch * 4], cst("TOT" + md, 64), gvf), r=["gv"], w=[("ps", 2)])
        S.op("act", ACT(gcs[:, 0:nch, :], g.bank(2)[0:64, 128:128 + nch * 4].rearrange("p (c x) -> p c x", x=4), AF.Copy),
             r=[("ps", 2)], w=["gcs"])
        S.op("act", ACT(gtots[:, 0:nch, :], g.bank(2)[0:64, 192:192 + nch * 4].rearrange("p (c x) -> p c x", x=4), AF.Copy),
             r=[("ps", 2)], w=["gtots"])
        if not sample:
            S.op("dve", TSC(rdl[:, 0:nch * 4], gcs[:, 0:nch, :], cst("LASTOHp", 64)[:, 0:1], ALU.mult), r=["gcs"], w=["rdl"])
        else:
            S.op("dve", TT(rdl[:, :].rearrange("p (s h) -> p s h", h=4), bc(gcs[:, 0, :], [64, NSS, 4], 1),
                           bc(cst("LASTOHs", 64), [64, NSS, 4], 2), ALU.mult), r=["gcs"], w=["rdl"])
        ndl = nch * NS * 4
        S.op("pe", MM(g.bank(2)[:, 256:256 + ndl], ones64, rdl[:, 0:ndl]), r=["rdl"], w=[("ps", 2)])
        S.op("act", ACT(dl[:, 0:ndl], g.bank(2)[:, 256:256 + ndl], AF.Exp), r=[("ps", 2)], w=["dl"])
        if sample:
            for c in range(12):
                bk = c % 2
                if c % 6 == 0:
                    S.op("sp", DMA(cio[:], g.conv0[l].rearrange("s j c -> (s j) c")[:, c * 128:c * 128 + 768]), w=["cio"], dma=True)
                S.op("pe", TR(g.bank(bk)[:, 0:48], cio[:, (c % 6) * 128:(c % 6 + 1) * 128], g.ident[0:48, 0:48]),
                     r=["cio"], w=[("ps", bk)])
                S.op("act", ACT(hists[:, c, :], g.bank(bk)[:, 0:48], AF.Copy), r=[("ps", bk)], w=[("hists", c)])
        for c in range(12):
            bk = c % 2
            pb = g.bank(bk)
            cb = g.bank(4 + bk)
            for k in range(KC):
                S.op("pe", MM(pb[:, 0:n], Wq[:, k, c * 128:(c + 1) * 128], hT[:, k, 0:n], start=(k == 0), stop=(k == KC - 1)),
                     r=[("Wq", c // 4), ("hT", "g")], w=[("ps", bk)])
            pc = pcc[bk]
            if not sample:
                S.op("pool", CP(pc[:, 0:3], hist[:, c, :]), r=["hist"], w=[("pcc", bk)])
                S.op("act", ACT(pc[:, 3:3 + n], pb[:, 0:n], AF.Copy), r=[("ps", bk)], w=[("pcc", bk)])
                if ti == 3:
                    S.op("dve", CP(cst32[:, c, 0:3], pb[:, n - 3:n]), r=[("ps", bk)], w=[("cst32", c)])
                for j in range(4):
                    S.op("pe", MM(cb[:, 0:n], cdiag[:, c * 4 + j, :], pc[:, j:j + n], start=(j == 0), stop=(j == 3)),
                         r=[("pcc", bk), ("cdiag", c * 4 + j)], w=[("ps", 4 + bk)])
                S.op("pool", CP(hist[:, c, :], pc[:, n:n + 3]), r=[("pcc", bk)], w=["hist"])
            else:
                pcs = pc[:, 0:NSS * 7].rearrange("p (s t) -> p s t", t=7)
                S.op("pool", CP(pcs[:, :, 0:3], hists[:, c, :].rearrange("p (s t) -> p s t", t=3)),
                     r=[("hists", c)], w=[("pcc", bk)])
                pbs = pb[:, 0:64].rearrange("p (s t) -> p s t", t=4)
                S.op("act", ACT(pcs[:, :, 3:7], pbs, AF.Copy), r=[("ps", bk)], w=[("pcc", bk)])
                S.op("dve", CP(cst32[:, c, :].rearrange("p (s t) -> p s t", t=3), pbs[:, :, 1:4]),
                     r=[("ps", bk)], w=[("cst32", c)])
                for j in range(4):
                    S.op("pe", MM(cb[:, 0:64].rearrange("p (s t) -> p s t", t=4), cdiag[:, c * 4 + j, :], pcs[:, :, j:j + 4],
                                  start=(j == 0), stop=(j == 3)),
                         r=[("pcc", bk), ("cdiag", c * 4 + j)], w=[("ps", 4 + bk)])
            S.op("act", ACT(qkvT[:, c, 0:n], cb[:, 0:n], AF.Silu), r=[("ps", 4 + bk)], w=[("qkvT", c)])
        if ti == 3 or sample:
            ncol = 48 if sample else 3
            cso = cio
            for c in range(12):
                bk = c % 2
                S.op("pe", TR(g.bank(bk)[0:ncol, 0:128], cst32[:, c, 0:ncol], g.ident[:, :]), r=[("cst32", c)], w=[("ps", bk)])
                S.op("act", ACT(cso[0:ncol, (c % 6) * 128:(c % 6 + 1) * 128], g.bank(bk)[0:ncol, 0:128], AF.Copy),
                     r=[("ps", bk)], w=["cio"])
                if c % 6 == 5:
                    c0_ = (c - 5) * 128
                    if sample:
                        S.op("sp", DMA(g.convs[l].rearrange("s j c -> (s j) c")[:, c0_:c0_ + 768], cso[0:48, :]), r=["cio"], dma=True)
                    else:
                        S.op("sp", DMA(g.convp[l][:, c0_:c0_ + 768], cso[0:3, :]), r=["cio"], dma=True)
        for ch in range(nch):
            S.op("act", ACT(sqk[:, :, :], qkvT[:, 0:8, ch * 64:(ch + 1) * 64], AF.Square), r=[("qkvT", c) for c in range(8)], w=["sqk"])
            for idx in range(8):
                S.op("pe", MM(g.bank(2)[0:64, 64 + ch * 8 + idx:64 + ch * 8 + idx + 1], sqk[:, idx, :],
                              g.onesb[:, 0:1]), r=["sqk"], w=[("ps", 2)])
        S.op("act", ACT(LN[:, 0:nch, :], g.bank(2)[0:64, 64:64 + nch * 8].rearrange("p (c x) -> p c x", x=8), AF.Ln,
                        bias=g.epsb[0:64, 0:1]), r=[("ps", 2)], w=["LN"])
        V = lambda s_: VEC[:, 0:nch, s_, :]
        S.op("dve", STT(V(0), LN[:, 0:nch, 4:8], -0.5, gcs[:, 0:nch, :], ALU.mult, ALU.subtract),
             r=["LN", "gcs"], w=[("VEC", 0)])
        S.op("dve", TT(tmpv[:, 0:nch, :], gcs[:, 0:nch, :], V(3), ALU.add), r=["gcs", ("VEC", 3)], w=["tmpv"])
        S.op("dve", STT(V(1), LN[:, 0:nch, 4:8], -0.5, tmpv[:, 0:nch, :], ALU.mult, ALU.add), r=["LN", "tmpv"], w=[("VEC", 1)])
        S.op("dve", STT(V(2), LN[:, 0:nch, 0:4], -0.5, gcs[:, 0:nch, :], ALU.mult, ALU.add), r=["LN", "gcs"], w=[("VEC", 2)])
        S.op("dve", TSC(V(2), V(2), LN_QSCALE, ALU.add), r=[("VEC", 2)], w=[("VEC", 2)])
        S.op("dve", TT(V(4), gtots[:, 0:nch, :], V(0), ALU.add), r=["gtots", ("VEC", 0)], w=[("VEC", 4)])
        S.op("act", ACT(SC[:, 0:nch, :, :], VEC[:, 0:nch, 1:5, :], AF.Exp), r=[("VEC", i) for i in range(1, 5)], w=["SC"])
        S.op("pool", CP(BV[:, 0:nch, 0, :], V(1)), r=[("VEC", 1)], w=["BV"])
        S.op("pool", CP(BV[:, 0:nch, 1, :], V(0)), r=[("VEC", 0)], w=["BV"])
        S.op("pool", CP(BV[:, 0:nch, 2, :], V(0)), r=[("VEC", 0)], w=["BV"])
        vecr = [("VEC", i) for i in range(5)]

        if sample:
            S.barrier()
            A2 = Arena(g.nc, reg0, reg1)
            qTm = [A2.alloc([128, NSS, 64], BF16, "qTm") for _ in range(2)]
            wTm = [A2.alloc([128, NSS, 64], BF16, "wTm") for _ in range(2)]
            kdm = [A2.alloc([64, NSS, 128], BF16, "kdm") for _ in range(2)]
            S0h = [A2.alloc([128, NSS, 128], F32, "S0h") for _ in range(2)]
            Sbh = [A2.alloc([128, NSS, 128], BF16, "Sbh") for _ in range(2)]
            smf = g.smf[:, :, :]

        for ch in range(nch):
            q0 = ch * 64
            for idx in range(8):
                S.op("pe", TR(b6[0:64, idx * 128:(idx + 1) * 128], qkvT[:, 4 + idx, q0:q0 + 64], g.identb[:, :]),
                     r=[("qkvT", 4 + idx)], w=[("ps", 6)])
            S.op("act", ACT(kvtm[:, :, :], b6[0:64, :].rearrange("p (a b) -> p a b", b=128), AF.Copy), r=[("ps", 6)], w=["kvtm"])
            for h in range(4):
                kT = qkvT[:, 4 + h, q0:q0 + 64]
                S.op("pe", MM(g.bank(3)[0:64, h * 64:(h + 1) * 64], kT, kT), r=[("qkvT", 4 + h)], w=[("ps", 3)])
            for h in range(4):
                S.op("pe", MM(g.bank(3)[0:64, 256 + h * 64:256 + (h + 1) * 64], qkvT[:, 4 + h, q0:q0 + 64],
                              qkvT[:, h, q0:q0 + 64]), r=[("qkvT", 4 + h), ("qkvT", h)], w=[("ps", 3)])
            S.op("dve", TT(RH[0:64, :].rearrange("p (a f) -> p a f", f=64), bc(ident64, [64, 12, 64], 1),
                           bc(VEC[:, ch, 0:3, :].rearrange("p a b -> p (a b)"), [64, 12, 64], 2), ALU.mult),
                 r=vecr[0:3], w=["RH_d"])
            S.op("pe", MM(g.bank(4)[0:64, 0:512], L2, RH[:, 0:512]), r=["RH_d", rhm], w=[("ps", 4)])
            S.op("pe", MM(g.bank(5)[0:64, 0:256], L2, RH[:, 512:768]), r=["RH_d", rhm], w=[("ps", 5)])
            S.op("dve", TT(X[:, 0:512].rearrange("p (a f) -> p a f", f=64), g.bank(4)[0:64, 0:512].rearrange("p (a f) -> p a f", f=64),
                           bc(BV[:, ch, 0:2, :].rearrange("p a b -> p (a b)"), [64, 8, 64], 2), ALU.add),
                 r=[("ps", 4), "BV"], w=["X"])
            S.op("dve", TT(X[:, 512:768].rearrange("p (a f) -> p a f", f=64), g.bank(5)[0:64, 0:256].rearrange("p (a f) -> p a f", f=64),
                           bc(BV[:, ch, 2, :], [64, 4, 64], 2), ALU.add), r=[("ps", 5), "BV"], w=["X"])
            S.op("act", ACT(X[:, :], X[:, :], AF.Exp), r=["X"], w=["X"])
            S.op("dve", STT(DCQ[0][:, 0:2, :, :].rearrange("p a h f -> p a (h f)"), X[:, 0:512].rearrange("p (a x) -> p a x", a=2),
                            -1.0, bc(g.bank(3)[0:64, 0:256], [64, 2, 256], 1), ALU.mult, ALU.mult),
                 r=["X", ("ps", 3)], w=[("DCQ", 0)])
            S.op("dve", TT(QKd[:, :, :].rearrange("p h f -> p (h f)"), X[:, 512:768], g.bank(3)[0:64, 256:512], ALU.mult),
                 r=["X", ("ps", 3)], w=["QKd"])
            S.op("dve", TT(DCQ[1][:, 2, :, :], bc(ident64, [64, 4, 64], 1), DCQ[0][:, 1, :, :], ALU.add),
                 r=[("DCQ", 0)], w=[("DCQ", 1)])
            for s_ in range(1, NL + 2):
                rb, wb = DCQ[(s_ - 1) % 2], DCQ[s_ % 2]
                rt, wt = ("DCQ", (s_ - 1) % 2), ("DCQ", s_ % 2)
                for h in range(4):
                    pbk = g.bank(4 + h // 2)
                    hh = h % 2
                    reg = lambda sl: pbk[0:64, sl * 128 + hh * 64: sl * 128 + hh * 64 + 64]
                    if s_ <= NL:
                        S.op("pe", MM(reg(0), rb[:, 1, h, :], rb[:, 0, h, :]), r=[rt], w=[("ps", 4 + h // 2)])
                        if s_ >= 2:
                            outap = pbk[0:64, 128:384].rearrange("p (a x) -> p a x", a=2)[:, :, hh * 64:hh * 64 + 64]
                            S.op("pe", MM(outap, rb[:, 0, h, :], rb[:, 1:3, h, :]), r=[rt], w=[("ps", 4 + h // 2)])
                        else:
                            S.op("pe", MM(reg(1), rb[:, 0, h, :], rb[:, 1, h, :]), r=[rt], w=[("ps", 4 + h // 2)])
                    else:
                        S.op("pe", MM(reg(2), rb[:, 0, h, :], rb[:, 2, h, :]), r=[rt], w=[("ps", 4 + h // 2)])
                for hp in range(2):
                    pbk = g.bank(4 + hp)
                    if s_ <= NL:
                        S.op("act", ACT(wb[:, 0:2, hp * 2:hp * 2 + 2, :], pbk[0:64, 0:256].rearrange("p (a h f) -> p a h f", a=2, h=2),
                                        AF.Copy), r=[("ps", 4 + hp)], w=[wt])
                    if s_ >= 2:
                        S.op("dve", TT(wb[:, 2, hp * 2:hp * 2 + 2, :], rb[:, 2, hp * 2:hp * 2 + 2, :],
                                       pbk[0:64, 256:384].rearrange("p (h f) -> p h f", h=2), ALU.add),
                             r=[rt, ("ps", 4 + hp)], w=[wt])
            fin = DCQ[(NL + 1) % 2]
            S.op("act", ACT(TTb[:, :, :], fin[:, 2, :, :], AF.Copy), r=[("DCQ", (NL + 1) % 2)], w=["TTb"])
            S.op("pool", TT(kbg[:, :, :], kvtm[:, 0:4, :], bc(SC[:, ch, 0, :], [64, 4, 128], 2), ALU.mult), r=["kvtm", "SC"], w=["kbg"])
            S.op("pool", TT(vb[:, :, :], kvtm[:, 4:8, :], bc(SC[:, ch, 2, :], [64, 4, 128], 2), ALU.mult), r=["kvtm", "SC"], w=["vb"])
            S.op("pool", TT(kd[:, :, :], kvtm[:, 0:4, :], bc(SC[:, ch, 3, :], [64, 4, 128], 2), ALU.mult), r=["kvtm", "SC"], w=["kd"])
            for h in range(4):
                S.op("pe", MM(g.bank(7)[:, h * 64:(h + 1) * 64], kbg[:, h, :], TTb[:, h, :]), r=["kbg", "TTb"], w=[("ps", 7)])
            for h in range(4):
                S.op("pe", MM(g.bank(3)[0:64, h * 128:(h + 1) * 128], TTb[:, h, :], vb[:, h, :]), r=["vb", "TTb"], w=[("ps", 3)])
            S.op("act", ACT(us[:, :, :], g.bank(3)[0:64, :].rearrange("p (h v) -> p h v", h=4), AF.Copy), r=[("ps", 3)], w=["us"])
            S.op("act", ACT(wTs[:, :, :], g.bank(7)[:, 0:256].rearrange("p (h f) -> p h f", h=4), AF.Copy), r=[("ps", 7)], w=["wTs"])
            if not sample:
                S.op("dve", TT(S32[:, :, :], S32[:, :, :], bc(dl[:, ch * 4:(ch + 1) * 4], [128, 4, 128], 2), ALU.mult),
                     r=["S32", "dl"], w=["S32"])
                for h in range(4):
                    S.op("pe", MM(g.bank(4)[0:64, h * 128:(h + 1) * 128], wTs[:, h, :], Sb[:, h, :]), r=["wTs", "Sb"], w=[("ps", 4)])
                S.op("dve", TT(vn[:, :, :], us[:, :, :], g.bank(4)[0:64, :].rearrange("p (h v) -> p h v", h=4), ALU.subtract),
                     r=["us", ("ps", 4)], w=["vn"])
                for h in range(4):
                    S.op("pe", MM(g.bank(5)[0:64, h * 128:(h + 1) * 128], qkvT[:, h, q0:q0 + 64], Sb[:, h, :]),
                         r=[("qkvT", h), "Sb"], w=[("ps", 5)])
                for h in range(4):
                    S.op("pe", MM(g.bank(7)[0:64, h * 128:(h + 1) * 128], QKd[:, h, :], vn[:, h, :]), r=["QKd", "vn"], w=[("ps", 7)])
                for h in range(4):
                    S.op("pe", MM(g.bank(3)[:, h * 128:(h + 1) * 128], kd[:, h, :], vn[:, h, :]), r=["kd", "vn"], w=[("ps", 3)])
                S.op("dve", TT(Sb[:, :, :], S32[:, :, :], g.bank(3)[:, :].rearrange("p (h v) -> p h v", h=4), ALU.add),
                     r=["S32", ("ps", 3)], w=["Sb"])
                S.op("dve", TT(S32[:, :, :], S32[:, :, :], g.bank(3)[:, :].rearrange("p (h v) -> p h v", h=4), ALU.add),
                     r=["S32", ("ps", 3)], w=["S32"])
            else:
                for h in range(4):
                    sl = h % 2
                    S.op("pool", TT(qTm[sl][:, :, :], bc(qkvT[:, h, 0:64], [128, NSS, 64], 1), smf, ALU.mult),
                         r=[("qkvT", h)], w=[("qTm", sl)])
                    S.op("dve", TT(wTm[sl][:, :, :], bc(wTs[:, h, :], [128, NSS, 64], 1), smf, ALU.mult),
                         r=["wTs"], w=[("wTm", sl)])
                    S.op("pool", TT(kdm[sl][:, :, :], bc(kd[:, h, :], [64, NSS, 128], 1), bc(cst("SEQOHs", 64), [64, NSS, 128], 2),
                                    ALU.mult), r=["kd"], w=[("kdm", sl)])
                    src = g.S0[l, :, h, :, :].rearrange("s k v -> k s v")
                    S.op("sp", DMA(S0h[sl][:], src), w=[("S0h", sl)], dma=True)
                    wload(g, Sbh[sl][:], src, ("Sbh", sl))
                    for s_ in range(NSS):
                        S.op("pe", MM(g.bank(4)[0:64, h * 128:(h + 1) * 128], wTm[sl][:, s_, :], Sbh[sl][:, s_, :],
                                      start=(s_ == 0), stop=(s_ == NSS - 1)), r=[("wTm", sl), ("Sbh", sl)], w=[("ps", 4)])
                    S.op("dve", TT(vn[:, h, :], us[:, h, :], g.bank(4)[0:64, h * 128:(h + 1) * 128], ALU.subtract),
                         r=["us", ("ps", 4)], w=[("vn", h)])
                    for s_ in range(NSS):
                        S.op("pe", MM(g.bank(5)[0:64, h * 128:(h + 1) * 128], qTm[sl][:, s_, :], Sbh[sl][:, s_, :],
                                      start=(s_ == 0), stop=(s_ == NSS - 1)), r=[("qTm", sl), ("Sbh", sl)], w=[("ps", 5)])
                    S.op("pe", MM(g.bank(7)[0:64, h * 128:(h + 1) * 128], QKd[:, h, :], vn[:, h, :]), r=["QKd", ("vn", h)], w=[("ps", 7)])
                    for grp in range(4):
                        pbk = 0 + (grp % 2)
                        for s4 in range(4):
                            s_ = grp * 4 + s4
                            S.op("pe", MM(g.bank(pbk)[:, s4 * 128:(s4 + 1) * 128], kdm[sl][:, s_, :], vn[:, h, :]),
                                 r=[("kdm", sl), ("vn", h)], w=[("ps", pbk)])
                        for s4 in range(4):
                            s_ = grp * 4 + s4
                            S.op("dve", STT(S0h[sl][:, s_, :], S0h[sl][:, s_, :], dl[:, s_ * 4 + h:s_ * 4 + h + 1],
                                            g.bank(pbk)[:, s4 * 128:(s4 + 1) * 128], ALU.mult, ALU.add),
                                 r=[("S0h", sl), "dl", ("ps", pbk)], w=[("S0h", sl)])
                    S.op("sp", DMA(g.Ss[l, :, h, :, :].rearrange("s k v -> k s v"), S0h[sl][:]), r=[("S0h", sl)], dma=True)
            vnr = [("vn", h) for h in range(4)] if sample else ["vn"]
            S.op("dve", TT(o32[:, :, :], g.bank(5)[0:64, :].rearrange("p (h v) -> p h v", h=4), bc(SC[:, ch, 1, :], [64, 4, 128], 2),
                           ALU.mult), r=[("ps", 5), "SC"], w=["o32"])
            S.op("dve", TT(o32[:, :, :], o32[:, :, :], g.bank(7)[0:64, :].rearrange("p (h v) -> p h v", h=4), ALU.add),
                 r=["o32", ("ps", 7)], w=["o32"])
            for h in range(4):
                S.op("act", ACT(X[:, 0:128], o32[:, h, :], AF.Square, accum=sso[:, h:h + 1]), r=["o32"], w=["X", ("sso", h)])
            S.op("act", ACT(sso[:, :], sso[:, :], AF.Ln, bias=g.epsb[0:64, 0:1], scale=1.0 / 128), r=[("sso", h) for h in range(4)],
                 w=[("sso", h) for h in range(4)])
            S.op("act", ACT(sso[:, :], sso[:, :], AF.Exp, scale=-0.5), r=[("sso", h) for h in range(4)], w=[("sso", h) for h in range(4)])
            S.op("pool", TT(on[:, :, :], o32[:, :, :], bc(sso[:, :], [64, 4, 128], 2), ALU.mult),
                 r=["o32"] + [("sso", h) for h in range(4)], w=["on"])
            for h in range(4):
                S.op("pe", TR(b6[:, h * 64:(h + 1) * 64], on[:, h, :], g.identb[0:64, 0:64]), r=["on"], w=[("ps", 6)])
            S.op("act", ACT(g.mixT[:, 0:4, t0 + q0:t0 + q0 + 64], b6[:, 0:256].rearrange("p (h f) -> p h f", h=4), AF.Copy,
                            scale=g.pvt[:, pvo + PV_GNORM:pvo + PV_GNORM + 1]), r=[("ps", 6)],
                 w=[("mixT", c_, ti) for c_ in range(4)])
        if ti == 3:
            S.op("sp", DMA(g.Sp[l].rearrange("h k v -> k h v"), S32[:, :, :]), r=["S32"], dma=True)
    S.barrier()
    A.release(m)


def pass_mlstm(g, l):
    S, A = g.S, g.A
    m = A.mark()
    pvo = PV_L * l
    cst = g.cst
    ident64 = cst("ident", 64)[:, 0:64]
    L2 = cst("L2")
    ones64 = cst("ones", 64)
    reg0 = A.mark()
    Wm = A.alloc([128, KC, 1024], BF16, "Wm")
    Wg = A.alloc([128, KC, 16], BF16, "Wg")
    win = wview(g.w_in[l], KC)
    for i in range(2):
        wload(g, Wm[:, :, i * 512:(i + 1) * 512], win[:, :, OFF_QM + i * 512:OFF_QM + (i + 1) * 512], ("Wm", i))
    wload(g, Wg[:], wview(g.wg[l], KC), "Wg")
    hT = A.alloc([128, KC, 512], BF16, "hT")
    rstd = A.alloc([128, 512], F32, "rstd")
    sqb = [A.alloc([128, 512], BF16, "sq") for _ in range(2)]
    reg1 = A.mark()
    qT = A.alloc([64, 4, 512], BF16, "qT")
    kT = A.alloc([64, 4, 512], BF16, "kT")
    vtm = A.alloc([64, 8, 4, 128], BF16, "vtm")
    ktm = A.alloc([64, 8, 4, 64], BF16, "ktm")
    G1 = A.alloc([64, 8, 8], F32, "G1")
    Lf = A.alloc([64, 8, 4], F32, "Lf")
    Fc = A.alloc([64, 8, 4], F32, "Fc")
    d1 = A.alloc([64, 8, 4], F32, "d1")
    RH1 = A.alloc([128, 256], F32, "RH1")
    RH2 = A.alloc([128, 256], F32, "RH2")
    mx = A.alloc([64, 4], F32, "mx")
    Mx = A.alloc([64, 4], F32, "Mx")
    mm2 = A.alloc([64, 8], F32, "mm2")
    mprev = A.alloc([64, 4], F32, "mprev")
    MxL = A.alloc([64, 4], F32, "MxL")
    negMx = A.alloc([64, 4], F32, "negMx")
    av = A.alloc([64, 4], F32, "av")
    enm = A.alloc([64, 4], F32, "enm")
    wC = A.alloc([64, 4], F32, "wC")
    tv = A.alloc([64, 4], F32, "tv")
    rdec = A.alloc([64, 64], F32, "rdec")
    dec = A.alloc([64, 64], F32, "dec")
    X2 = A.alloc([64, 256], F32, "X2")
    SmT = A.alloc([64, 4, 64], BF16, "SmT")
    P2s = A.alloc([64, 4, 128], F32, "P2s")
    num = A.alloc([64, 4, 128], F32, "num")
    hn = A.alloc([64, 4, 128], BF16, "hn")
    dd = A.alloc([64, 4], F32, "dd")
    rden = A.alloc([64, 4], F32, "rden")
    ssn = A.alloc([64, 4], F32, "ssn")
    kw = A.alloc([64, 4, 64], BF16, "kw")
    C32 = A.alloc([64, 4, 128], F32, "C32")
    Cb = A.alloc([64, 4, 128], BF16, "Cb")
    n32 = A.alloc([64, 4], F32, "n32")
    nb = A.alloc([64, 4], BF16, "nb")
    mo = A.alloc([16, 4], F32, "mo")
    nio = A.alloc([64, 64], F32, "nio")
    S.op("dve", MSET(C32[:], 0.0), w=["C32"])
    S.op("dve", MSET(Cb[:], 0.0), w=["Cb"])
    S.op("dve", MSET(n32[:], 0.0), w=["n32"])
    S.op("dve", MSET(nb[:], 0.0), w=["nb"])
    S.op("dve", MSET(mprev[:], 0.0), w=["mprev"])
    b6 = g.psb6
    b2 = g.bank(2)

    for ti, (t0, n) in enumerate(TILES):
        sample = (ti == 4)
        nch = n // 64
        md = "s" if sample else "p"
        NS = NSS if sample else 1
        if ti == 0 or sample:
            o_, n_ = _BCOL["M1" + md]
            S.op("sp", DMA(RH1[64:128, :], g.cB[:, o_:o_ + n_]), w=["RH1_m"], dma=True)
            o_, n_ = _BCOL["M2" + md]
            S.op("sp", DMA(RH2[64:128, :], g.cB[:, o_:o_ + n_]), w=["RH2_m"], dma=True)
        rmsnorm_tile(g, hT, 0, t0, n, pvo + PV_NMIX, sqb, rstd, 7, "m")
        for ch in range(nch):
            for k in range(KC):
                S.op("pe", MM(b2[0:64, ch * 8:(ch + 1) * 8], hT[:, k, ch * 64:(ch + 1) * 64], Wg[:, k, 8:16],
                              start=(k == 0), stop=(k == KC - 1)), r=[("hT", "m"), "Wg"], w=[("ps", 2)])
        gps = b2[0:64, 0:nch * 8].rearrange("p (c x) -> p c x", x=8)
        S.op("dve", TT(G1[:, 0:nch, :], gps, bc(g.pvt[0:64, pvo + PV_GBIAS + 8:pvo + PV_GBIAS + 16], [64, nch, 8], 1), ALU.add),
             r=[("ps", 2)], w=["G1"])
        S.op("act", ACT(Lf[:, 0:nch, :], G1[:, 0:nch, 4:8], AF.Exp, scale=-1.0), r=["G1"], w=["Lf"])
        S.op("act", ACT(Lf[:, 0:nch, :], Lf[:, 0:nch, :], AF.Ln, bias=g.oneb[0:64, 0:1]), r=["Lf"], w=["Lf"])
        S.op("pe", MM(b2[0:64, 128:128 + nch * 4], cst("CUM" + md, 64), Lf[:, 0:nch, :]), r=["Lf"], w=[("ps", 2)])
        cps = b2[0:64, 128:128 + nch * 4].rearrange("p (c x) -> p c x", x=4)
        S.op("dve", TSC(Fc[:, 0:nch, :], cps, -1.0, ALU.mult), r=[("ps", 2)], w=["Fc"])
        S.op("dve", TT(d1[:, 0:nch, :], G1[:, 0:nch, 0:4], cps, ALU.add), r=["G1", ("ps", 2)], w=["d1"])
        for qk in range(2):
            dst = qT if qk == 0 else kT
            for h in range(4):
                bk = h % 2
                col = qk * 256 + h * 64
                for k in range(KC):
                    S.op("pe", MM(g.bank(bk)[0:64, 0:n], Wm[:, k, col:col + 64], hT[:, k, 0:n], start=(k == 0), stop=(k == KC - 1)),
                         r=[("Wm", 0), ("hT", "m")], w=[("ps", bk)])
                S.op("act", ACT(dst[:, h, 0:n], g.bank(bk)[0:64, 0:n], AF.Copy, scale=(0.125 if qk == 0 else 1.0)),
                     r=[("ps", bk)], w=[("qkT", qk, h)])
        for ch in range(nch):
            for k in range(KC):
                S.op("pe", MM(g.bank(0)[0:64, 0:512], hT[:, k, ch * 64:(ch + 1) * 64], Wm[:, k, 512:1024],
                              start=(k == 0), stop=(k == KC - 1)), r=[("Wm", 1), ("hT", "m")], w=[("ps", 0)])
            S.op("act", ACT(vtm[:, ch, :, :], g.bank(0)[0:64, 0:512].rearrange("p (h v) -> p h v", h=4), AF.Copy),
                 r=[("ps", 0)], w=[("vtm", ch)])
            for k in range(KC):
                S.op("pe", MM(g.bank(1)[0:64, 0:256], hT[:, k, ch * 64:(ch + 1) * 64], Wm[:, k, 256:512],
                              start=(k == 0), stop=(k == KC - 1)), r=[("Wm", 0), ("hT", "m")], w=[("ps", 1)])
            S.op("dve", CP(ktm[:, ch, :, :], g.bank(1)[0:64, 0:256].rearrange("p (h v) -> p h v", h=4)),
                 r=[("ps", 1)], w=[("ktm", ch)])
        if sample:
            A2 = A
            qTm = [A2.alloc([64, NSS, 64], BF16, "qTm") for _ in range(2)]
            kwm = [A2.alloc([64, NSS, 64], BF16, "kwm") for _ in range(2)]
            C0h = [A2.alloc([64, NSS, 128], F32, "C0h") for _ in range(2)]
            Cbh = [A2.alloc([64, NSS, 128], BF16, "Cbh") for _ in range(2)]
            n0t = A2.alloc([64, NSS, 4], F32, "n0t")
            n0b = A2.alloc([64, NSS, 4], BF16, "n0b")
            m0t = A2.alloc([16, 4], F32, "m0t")
            smf = g.smf[0:64, :, :]
            S.op("sp", DMA(m0t[:], g.m0[l]), w=["m0t"], dma=True)
            S.op("pe", MM(b2[0:64, 200:204], cst("EXPAND", 16), m0t[:, :]), r=["m0t"], w=[("ps", 2)])
            S.op("act", ACT(mprev[:, :], b2[0:64, 200:204], AF.Copy), r=[("ps", 2)], w=["mprev"])
            S.op("sp", DMA(nio[:], g.n0[l].rearrange("s h k -> (s h) k")), w=["nio"], dma=True)
            S.op("pe", TR(g.bank(3)[0:64, 0:64], nio[:, :], ident64), r=["nio"], w=[("ps", 3)])
            S.op("act", ACT(n0t[:, :, :].rearrange("p s h -> p (s h)"), g.bank(3)[0:64, 0:64], AF.Copy), r=[("ps", 3)], w=["n0t"])
            S.op("dve", CP(n0b[:, :, :], n0t[:, :, :]), r=["n0t"], w=["n0b"])

        for ch in range(nch):
            q0 = ch * 64
            S.op("dve", TT(RH1[0:64, :].rearrange("p (h f) -> p h f", f=64), bc(ident64, [64, 4, 64], 1),
                           bc(d1[:, ch, :], [64, 4, 64], 2), ALU.mult), r=["d1"], w=["RH1_d"])
            S.op("pe", MM(g.bank(4)[0:64, 0:256], L2, RH1[:, :]), r=["RH1_d", "RH1_m"], w=[("ps", 4)])
            S.op("dve", RED(mx[:, :], g.bank(4)[0:64, 0:256].rearrange("p (h f) -> p h f", f=64), ALU.max), r=[("ps", 4)], w=["mx"])
            S.op("dve", TT(mm2[:, 4:8], mx[:, :], mprev[:, :], ALU.max), r=["mx", "mprev"], w=["Mx"])
            S.op("dve", TT(mm2[:, 0:4], Fc[:, ch, :], mm2[:, 4:8], ALU.add), r=["Fc", "Mx"], w=["mrow"])
            S.op("dve", TT(tv[:, :], mprev[:, :], mm2[:, 4:8], ALU.subtract), r=["mprev", "Mx"], w=["tv"])
            S.op("act", ACT(av[:, :], tv[:, :], AF.Exp), r=["tv"], w=["av"])
            S.op("act", ACT(enm[:, :], mm2[:, 0:4], AF.Exp, scale=-1.0), r=["mrow"], w=["enm"])
            S.op("dve", TSC(negMx[:, :], mm2[:, 4:8], -1.0, ALU.mult), r=["Mx"], w=["negMx"])
            S.op("pe", MM(b2[0:64, 208:216], cst("LASTSEL" + md, 64), mm2[:, :]), r=["mrow", "Mx"], w=[("ps", 2)])
            if not sample:
                S.op("dve", TSC(rdec[:, 0:4], tv[:, :], cst("LASTOHp", 64)[:, 0:1], ALU.mult), r=["tv"], w=["rdec"])
            else:
                S.op("dve", TT(rdec[:, :].rearrange("p (s h) -> p s h", h=4), bc(tv[:, :], [64, NSS, 4], 1),
                               bc(cst("LASTOHs", 64), [64, NSS, 4], 2), ALU.mult), r=["tv"], w=["rdec"])
            S.op("pe", MM(b2[0:64, 256:256 + NS * 4], ones64[:, 0:64], rdec[:, 0:NS * 4]), r=["rdec"], w=[("ps", 2)])
            S.op("act", ACT(dec[:, 0:NS * 4], b2[0:64, 256:256 + NS * 4], AF.Exp), r=[("ps", 2)], w=["dec"])
            if (ti == 3 and ch == nch - 1) or sample:
                S.op("pe", MM(b2[0:NS, 220:224], cst("LASTOH" + md, 64), mm2[:, 0:4]), r=["mrow"], w=[("ps", 2)])
                S.op("act", ACT(mo[0:NS, :], b2[0:NS, 220:224], AF.Copy), r=[("ps", 2)], w=["mo"])
                if sample:
                    S.op("sp", DMA(g.ms[l], mo[0:NSS, :]), r=["mo"], dma=True)
                else:
                    S.op("sp", DMA(g.mp[l:l + 1, :], mo[0:1, :]), r=["mo"], dma=True)
            S.op("dve", TT(wC[:, :], d1[:, ch, :], b2[0:64, 212:216], ALU.subtract), r=["d1", ("ps", 2)], w=["wC"])
            S.op("act", ACT(wC[:, :], wC[:, :], AF.Exp), r=["wC"], w=["wC"])
            S.op("act", ACT(mprev[:, :], b2[0:64, 208:212], AF.Copy), r=[("ps", 2), "tv", "Mx"], w=["mprev"])
            for h in range(4):
                S.op("pe", MM(g.bank(3)[0:64, h * 64:(h + 1) * 64], kT[:, h, q0:q0 + 64], qT[:, h, q0:q0 + 64]),
                     r=[("qkT", 0, h), ("qkT", 1, h)], w=[("ps", 3)])
            S.op("dve", TT(RH2[0:64, :].rearrange("p (h f) -> p h f", f=64), bc(ident64, [64, 4, 64], 1),
                           bc(negMx[:, :], [64, 4, 64], 2), ALU.mult), r=["negMx"], w=["RH2_d"])
            S.op("pe", MM(g.bank(4)[0:64, 256:512], L2, RH2[:, :]), r=["RH2_d", "RH2_m"], w=[("ps", 4)])
            S.op("dve", TT(X2[:, :].rearrange("p (h f) -> p h f", f=64), g.bank(4)[0:64, 256:512].rearrange("p (h f) -> p h f", f=64),
                           bc(d1[:, ch, :], [64, 4, 64], 2), ALU.add), r=[("ps", 4), "d1"], w=["X2"])
            S.op("act", ACT(X2[:, :], X2[:, :], AF.Exp), r=["X2"], w=["X2"])
            S.op("dve", TT(SmT[:, :, :].rearrange("p h f -> p (h f)"), X2[:, :], g.bank(3)[0:64, 0:256], ALU.mult),
                 r=["X2", ("ps", 3)], w=["SmT"])
            for h in range(4):
                S.op("pe", MM(g.bank(5)[0:64, h * 128:(h + 1) * 128], SmT[:, h, :], vtm[:, ch, h, :]), r=["SmT", ("vtm", ch)], w=[("ps", 5)])
            for h in range(4):
                S.op("pe", MM(b2[0:64, 240 + h:241 + h], SmT[:, h, :], g.onesb[0:64, 0:1]), r=["SmT"], w=[("ps", 2)])
            S.op("act", ACT(P2s[:, :, :], g.bank(5)[0:64, :].rearrange("p (h v) -> p h v", h=4), AF.Copy), r=[("ps", 5)], w=["P2s"])
            S.op("pool", TT(kw[:, :, :], ktm[:, ch, :, :], bc(wC[:, :], [64, 4, 64], 2), ALU.mult), r=[("ktm", ch), "wC"], w=["kw"])
            if not sample:
                for h in range(4):
                    S.op("pe", MM(g.bank(7)[0:64, h * 128:(h + 1) * 128], qT[:, h, q0:q0 + 64], Cb[:, h, :]),
                         r=[("qkT", 0, h), "Cb"], w=[("ps", 7)])
                for h in range(4):
                    S.op("pe", MM(b2[0:64, 244 + h:245 + h], qT[:, h, q0:q0 + 64], nb[:, h:h + 1]), r=[("qkT", 0, h), "nb"], w=[("ps", 2)])
                S.op("dve", TT(C32[:, :, :], C32[:, :, :], bc(dec[:, 0:4], [64, 4, 128], 2), ALU.mult), r=["C32", "dec"], w=["C32"])
                for h in range(4):
                    S.op("pe", MM(g.bank(3)[0:64, h * 128:(h + 1) * 128], kw[:, h, :], vtm[:, ch, h, :]), r=["kw", ("vtm", ch)], w=[("ps", 3)])
                for h in range(4):
                    S.op("pe", MM(b2[0:64, 248 + h:249 + h], kw[:, h, :], g.onesb[0:64, 0:1]), r=["kw"], w=[("ps", 2)])
                S.op("dve", TT(Cb[:, :, :], C32[:, :, :], g.bank(3)[0:64, :].rearrange("p (h v) -> p h v", h=4), ALU.add),
                     r=["C32", ("ps", 3)], w=["Cb"])
                S.op("dve", TT(C32[:, :, :], C32[:, :, :], g.bank(3)[0:64, :].rearrange("p (h v) -> p h v", h=4), ALU.add),
                     r=["C32", ("ps", 3)], w=["C32"])
                S.op("dve", TT(n32[:, :], n32[:, :], dec[:, 0:4], ALU.mult), r=["n32", "dec"], w=["n32"])
                S.op("dve", TT(n32[:, :], n32[:, :], b2[0:64, 248:252], ALU.add), r=["n32", ("ps", 2)], w=["n32"])
                S.op("dve", CP(nb[:, :], n32[:, :]), r=["n32"], w=["nb"])
            else:
                nnew = A2.alloc([64, NSS, 4], F32, "nnew")
                for h in range(4):
                    sl = h % 2
                    S.op("pool", TT(qTm[sl][:, :, :], bc(qT[:, h, 0:64], [64, NSS, 64], 1), smf, ALU.mult),
                         r=[("qkT", 0, h)], w=[("qTm", sl)])
                    S.op("pool", TT(kwm[sl][:, :, :], bc(kw[:, h, :], [64, NSS, 64], 1), bc(cst("SEQOHs", 64), [64, NSS, 64], 2),
                                    ALU.mult), r=["kw"], w=[("kwm", sl)])
                    src = g.C0[l, :, h, :, :].rearrange("s k v -> k s v")
                    S.op("sp", DMA(C0h[sl][:], src), w=[("C0h", sl)], dma=True)
                    wload(g, Cbh[sl][:], src, ("Cbh", sl))
                    for s_ in range(NSS):
                        S.op("pe", MM(g.bank(7)[0:64, h * 128:(h + 1) * 128], qTm[sl][:, s_, :], Cbh[sl][:, s_, :],
                                      start=(s_ == 0), stop=(s_ == NSS - 1)), r=[("qTm", sl), ("Cbh", sl)], w=[("ps", 7)])
                    for s_ in range(NSS):
                        S.op("pe", MM(b2[0:64, 244 + h:245 + h], qTm[sl][:, s_, :], n0b[:, s_, h:h + 1],
                                      start=(s_ == 0), stop=(s_ == NSS - 1)), r=[("qTm", sl), "n0b"], w=[("ps", 2)])
                    for grp in range(4):
                        pbk = grp % 2
                        for s4 in range(4):
                            s_ = grp * 4 + s4
                            S.op("pe", MM(g.bank(pbk)[0:64, s4 * 128:(s4 + 1) * 128], kwm[sl][:, s_, :], vtm[:, 0, h, :]),
                                 r=[("kwm", sl), ("vtm", 0)], w=[("ps", pbk)])
                        for s4 in range(4):
                            s_ = grp * 4 + s4
                            S.op("dve", STT(C0h[sl][:, s_, :], C0h[sl][:, s_, :], dec[:, s_ * 4 + h:s_ * 4 + h + 1],
                                            g.bank(pbk)[0:64, s4 * 128:(s4 + 1) * 128], ALU.mult, ALU.add),
                                 r=[("C0h", sl), "dec", ("ps", pbk)], w=[("C0h", sl)])
                    S.op("sp", DMA(g.Cs[l, :, h, :, :].rearrange("s k v -> k s v"), C0h[sl][:]), r=[("C0h", sl)], dma=True)
                    for s_ in range(NSS):
                        S.op("pe", MM(b2[0:64, 384 + h * 16 + s_:385 + h * 16 + s_], kwm[sl][:, s_, :], g.onesb[0:64, 0:1]),
                             r=[("kwm", sl)], w=[("ps", 2)])
                S.op("dve", TT(nnew[:, :, :], n0t[:, :, :], dec[:, 0:64].rearrange("p (s h) -> p s h", h=4), ALU.mult),
                     r=["n0t", "dec"], w=["nnew"])
                S.op("dve", TT(nnew[:, :, :], nnew[:, :, :], b2[0:64, 384:448].rearrange("p (h s) -> p s h", h=4), ALU.add),
                     r=["nnew", ("ps", 2)], w=["nnew"])
                S.op("pe", TR(g.bank(3)[0:64, 0:64], nnew[:, :, :].rearrange("p s h -> p (s h)"), ident64), r=["nnew"], w=[("ps", 3)])
                S.op("act", ACT(nio[:, :], g.bank(3)[0:64, 0:64], AF.Copy), r=[("ps", 3)], w=["nio"])
                S.op("sp", DMA(g.ns[l].rearrange("s h k -> (s h) k"), nio[:, :]), r=["nio"], dma=True)
            S.op("dve", TT(num[:, :, :], g.bank(7)[0:64, :].rearrange("p (h v) -> p h v", h=4), bc(av[:, :], [64, 4, 128], 2), ALU.mult),
                 r=[("ps", 7), "av"], w=["num"])
            S.op("dve", TT(num[:, :, :], num[:, :, :], P2s[:, :, :], ALU.add), r=["num", "P2s"], w=["num"])
            S.op("dve", TT(dd[:, :], av[:, :], b2[0:64, 244:248], ALU.mult), r=["av", ("ps", 2)], w=["dd"])
            S.op("dve", TT(dd[:, :], dd[:, :], b2[0:64, 240:244], ALU.add), r=["dd", ("ps", 2)], w=["dd"])
            S.op("dve", TSC(rden[:, :], dd[:, :], -1.0, ALU.mult), r=["dd"], w=["rden"])
            S.op("dve", TT(dd[:, :], dd[:, :], rden[:, :], ALU.max), r=["dd", "rden"], w=["dd"])
            S.op("dve", TT(dd[:, :], dd[:, :], enm[:, :], ALU.max), r=["dd", "enm"], w=["dd"])
            S.op("dve", RCP(rden[:, :], dd[:, :]), r=["dd"], w=["rden"])
            for h in range(4):
                S.op("act", ACT(X2[:, 0:128], num[:, h, :], AF.Square, accum=ssn[:, h:h + 1]), r=["num"], w=["X2", ("ssn", h)])
            ssr = [("ssn", h) for h in range(4)]
            S.op("dve", TT(ssn[:, :], ssn[:, :], rden[:, :], ALU.mult), r=ssr + ["rden"], w=ssr)
            S.op("dve", TT(ssn[:, :], ssn[:, :], rden[:, :], ALU.mult), r=ssr + ["rden"], w=ssr)
            S.op("act", ACT(ssn[:, :], ssn[:, :], AF.Ln, bias=g.epsb[0:64, 0:1], scale=1.0 / 128), r=ssr, w=ssr)
            S.op("act", ACT(ssn[:, :], ssn[:, :], AF.Exp, scale=-0.5), r=ssr, w=ssr)
            S.op("dve", TT(ssn[:, :], ssn[:, :], rden[:, :], ALU.mult), r=ssr + ["rden"], w=ssr)
            S.op("pool", TT(hn[:, :, :], num[:, :, :], bc(ssn[:, :], [64, 4, 128], 2), ALU.mult), r=["num"] + ssr, w=["hn"])
            for h in range(4):
                S.op("pe", TR(b6[:, h * 64:(h + 1) * 64], hn[:, h, :], g.identb[0:64, 0:64]), r=["hn"], w=[("ps", 6)])
            S.op("dve", TT(g.mixT[:, 4:8, t0 + q0:t0 + q0 + 64], b6[:, 0:256].rearrange("p (h f) -> p h f", h=4),
                           bc(g.pvt[:, pvo + PV_MLNORM:pvo + PV_MLNORM + 4], [128, 4, 64], 2), ALU.mult), r=[("ps", 6)],
                 w=[("mixT", 4 + c_, ti) for c_ in range(4)])
        if ti == 3:
            S.op("sp", DMA(g.Cp[l].rearrange("h k v -> k h v"), C32[:, :, :]), r=["C32"], dma=True)
            S.op("pe", TR(g.bank(3)[0:4, 0:64], n32[:, :], ident64), r=["n32"], w=[("ps", 3)])
            S.op("act", ACT(nio[0:4, :], g.bank(3)[0:4, 0:64], AF.Copy), r=[("ps", 3)], w=["nio"])
            S.op("sp", DMA(g.np_[l], nio[0:4, :]), r=["nio"], dma=True)
    S.barrier()
    A.release(m)
```

```python
import os
import numpy as np
import concourse.bass as bass
import concourse.mybir as mybir
from concourse.bass_utils import run_bass_kernel_spmd

F32 = mybir.dt.float32
BF16 = mybir.dt.bfloat16
AF = mybir.ActivationFunctionType
ALU = mybir.AluOpType
AX = mybir.AxisListType

NCORES = 8
D = 1024
KC = 8
SEQ = 2048
NSS = 16
TS_ = 4
T = SEQ + NSS * TS_
DEPTH = 2
IN_DIM = 5648
DFF = 4096
EPS = 1e-6
NEG = -30000.0
SEM_LIMIT = 8000
TILES = [(0, 512), (512, 512), (1024, 512), (1536, 512), (2048, 64)]


class Sched:
    ENGS = ("pe", "act", "dve", "pool", "sp")

    def __init__(self, nc):
        self.nc = nc
        self.ops = {e: [] for e in self.ENGS}
        self.csem = {}
        self.ccnt = {}
        self.nsem = 0
        for e in self.ENGS:
            self.csem[e] = self._newsem("c_" + e)
            self.ccnt[e] = 0
        self.dsem = {}
        self.dcnt = {}
        self.drr = {}
        for q in ("sp", "pool", "act"):
            self.dsem[q] = [self._newsem("d_%s%d" % (q, i)) for i in range(8)]
            self.dcnt[q] = [0] * 8
            self.drr[q] = 0
        self.last_w = {}
        self.readers = {}
        self.seen = {e: {} for e in self.ENGS}
        self.pending = {e: [] for e in self.ENGS}
        self.latest = {}

    def _newsem(self, name):
        self.nsem += 1
        return self.nc.alloc_semaphore("%s_%d" % (name, self.nsem))

    def op(self, eng, fn, r=(), w=(), dma=False):
        deps = {}

        def add(cid, kind):
            if cid is None:
                return
            s, v, peng = cid
            if eng == "pe" and peng == "pe":
                return
            if peng == eng and kind == "war" and not dma:
                return
            k = id(s)
            if k not in deps or deps[k][1] < v:
                deps[k] = (s, v)

        for t in r:
            add(self.last_w.get(t), "raw")
        for t in w:
            add(self.last_w.get(t), "waw")
            for c in self.readers.get(t, ()):
                add(c, "war")
        waits = list(self.pending[eng])
        self.pending[eng] = []
        for k, (s, v) in deps.items():
            if self.seen[eng].get(k, 0) < v:
                waits.append((s, v))
                self.seen[eng][k] = v
        if dma:
            q = eng
            i = self.drr[q]
            self.drr[q] = (i + 1) % len(self.dsem[q])
            if self.dcnt[q][i] + 16 > SEM_LIMIT:
                s_old = self.dsem[q][i]
                if self.seen[eng].get(id(s_old), 0) < self.dcnt[q][i]:
                    waits.append((s_old, self.dcnt[q][i]))
                    self.seen[eng][id(s_old)] = self.dcnt[q][i]
                self.dsem[q][i] = self._newsem("d_" + q)
                self.dcnt[q][i] = 0
            s = self.dsem[q][i]
            if self.dcnt[q][i] > 0 and self.seen[eng].get(id(s), 0) < self.dcnt[q][i]:
                waits.append((s, self.dcnt[q][i]))
                self.seen[eng][id(s)] = self.dcnt[q][i]
            self.dcnt[q][i] += 16
            cid = (s, self.dcnt[q][i], "dma")
            inc = (s, 16)
        else:
            if self.ccnt[eng] + 1 > SEM_LIMIT:
                self.csem[eng] = self._newsem("c_" + eng)
                self.ccnt[eng] = 0
            self.ccnt[eng] += 1
            s = self.csem[eng]
            cid = (s, self.ccnt[eng], eng)
            inc = (s, 1)
        self.latest[id(cid[0])] = (cid[0], cid[1])
        self.ops[eng].append((waits, fn, inc))
        for t in r:
            self.readers.setdefault(t, []).append(cid)
        for t in w:
            self.last_w[t] = cid
            self.readers[t] = []
        return cid

    def barrier(self):
        for e in self.ENGS:
            for k, (s, v) in self.latest.items():
                if self.seen[e].get(k, 0) < v:
                    self.pending[e].append((s, v))
                    self.seen[e][k] = v
        self.last_w = {}
        self.readers = {}

    def emit(self):
        nc = self.nc
        fin = []
        for k, (s, v) in self.latest.items():
            if self.seen["sp"].get(k, 0) < v:
                fin.append((s, v))
        fin = self.pending["sp"] + fin
        ops = self.ops

        def run(e, name, extra=()):
            for waits, fn, inc in ops[name]:
                for s, v in waits:
                    e.wait_ge(s, v)
                ins = fn(e)
                ins.then_inc(inc[0], inc[1])
            for s, v in extra:
                e.wait_ge(s, v)

        with nc.Block() as block:
            @block.tensor
            def _(e):
                run(e, "pe")

            @block.scalar
            def _(e):
                run(e, "act")

            @block.vector
            def _(e):
                run(e, "dve")

            @block.gpsimd
            def _(e):
                run(e, "pool")

            @block.sync
            def _(e):
                run(e, "sp", fin)


class Arena:
    def __init__(self, nc, base, top):
        self.nc = nc
        self.off = base
        self.top = top
        self.n = 0
        self.peak = base

    def alloc(self, shape, dtype, name="t"):
        nb = 1
        for s in shape[1:]:
            nb *= s
        nb *= 4 if dtype == F32 else 2
        off = (self.off + 63) // 64 * 64
        assert off + nb <= self.top, "SBUF arena overflow: %s %s need %d have %d" % (name, shape, nb, self.top - off)
        self.off = off + nb
        self.peak = max(self.peak, self.off)
        self.n += 1
        return self.nc.alloc_sbuf_tensor_at("%s_%d" % (name, self.n), list(shape), dtype, offset=off)

    def mark(self):
        return self.off

    def release(self, m):
        self.off = m


def MM(out, lhsT, rhs, start=True, stop=True):
    return lambda e: e.matmul(out, lhsT=lhsT, rhs=rhs, start=start, stop=stop)


def TR(out, in_, ident):
    return lambda e: e.transpose(out, in_, ident)


def TT(out, a, b, op):
    return lambda e: e.tensor_tensor(out=out, in0=a, in1=b, op=op)


def TSC(out, a, s1, op0, s2=None, op1=None):
    if op1 is None:
        return lambda e: e.tensor_scalar(out=out, in0=a, scalar1=s1, scalar2=None, op0=op0)
    return lambda e: e.tensor_scalar(out=out, in0=a, scalar1=s1, scalar2=s2, op0=op0, op1=op1)


def STT(out, a, s, b, op0, op1):
    return lambda e: e.scalar_tensor_tensor(out=out, in0=a, scalar=s, in1=b, op0=op0, op1=op1)


def ACT(out, in_, func, bias=None, scale=None, accum=None):
    kw = {}
    if bias is not None:
        kw["bias"] = bias
    if scale is not None:
        kw["scale"] = scale
    if accum is not None:
        kw["accum_out"] = accum
    return lambda e: e.activation(out=out, in_=in_, func=func, **kw)


def CP(out, in_):
    return lambda e: e.tensor_copy(out=out, in_=in_)


def MSET(ap, v):
    return lambda e: e.memset(ap, v)


def RED(out, in_, op, axis=AX.X):
    return lambda e: e.tensor_reduce(out=out, in_=in_, axis=axis, op=op)


def RCP(out, in_):
    return lambda e: e.reciprocal(out=out, in_=in_)


def DMA(out, in_):
    return lambda e: e.dma_start(out=out, in_=in_)


def bc(ap, shape, axis):
    return ap.unsqueeze(axis).to_broadcast(list(shape))


def _seq_of(mode):
    if mode == "p":
        return np.zeros(64, np.int64)
    return np.arange(64) // TS_


def build_consts():
    cols = {}
    parts = []
    off = [0]

    def put(name, arr):
        arr = np.asarray(arr, np.float32)
        a = np.zeros((128, arr.shape[1]), np.float32)
        a[: arr.shape[0]] = arr
        cols[name] = (off[0], arr.shape[1])
        parts.append(a)
        off[0] += arr.shape[1]

    put("ident", np.eye(128))
    L2 = np.zeros((128, 64), np.float32)
    L2[:64] = 1.0
    L2[64:] = np.eye(64)
    put("L2", L2)
    put("ones", np.ones((128, 128)))
    maskB = {}
    for mode in ("p", "s"):
        sq = _seq_of(mode)
        k = np.arange(64)
        same = sq[:, None] == sq[None, :]
        last = np.array([np.max(np.nonzero(sq == sq[t])[0]) for t in range(64)])
        cum = ((k[:, None] <= k[None, :]) & same).astype(np.float32)
        tot = same.astype(np.float32)
        lastsel = (k[:, None] == last[None, :]).astype(np.float32)
        put("CUM" + mode, cum)
        put("TOT" + mode, tot)
        put("LASTSEL" + mode, lastsel)
        ns = 1 if mode == "p" else NSS
        seqoh = (sq[:, None] == np.arange(ns)[None, :]).astype(np.float32)
        lastoh = seqoh * (k == last)[:, None]
        put("SEQOH" + mode, seqoh)
        put("LASTOH" + mode, lastoh)
        lo_s = np.where((k[:, None] > k[None, :]) & same, 0.0, NEG)
        lo_i = np.where((k[:, None] >= k[None, :]) & same, 0.0, NEG)
        up_s = np.where((k[None, :] > k[:, None]) & same, 0.0, NEG)
        up_i = np.where((k[None, :] >= k[:, None]) & same, 0.0, NEG)
        rep = lambda m: np.repeat(m[:, None, :], 4, axis=1).reshape(64, 256)
        maskB["G" + mode] = np.concatenate([rep(lo_s), rep(up_s), rep(up_i)], axis=1)
        maskB["M1" + mode] = rep(lo_i)
        maskB["M2" + mode] = rep(up_i)
    sq = _seq_of("s")
    smf = (sq[None, :] == np.arange(NSS)[:, None]).astype(np.float32).reshape(1, NSS * 64)
    cC = np.repeat(smf, 128, axis=0).astype(np.float32)
    put("EXPAND", (np.arange(NSS)[:, None] == sq[None, :]).astype(np.float32))
    cA = np.concatenate(parts, axis=1)
    names = ["Gp", "Gs", "M1p", "M1s", "M2p", "M2s"]
    bcols = {}
    o = 0
    bl = []
    for n in names:
        bcols[n] = (o, maskB[n].shape[1])
        o += maskB[n].shape[1]
        bl.append(maskB[n])
    cB = np.concatenate(bl, axis=1).astype(np.float32)
    return cA, cB, cC, cols, bcols


_CA, _CB, _CC, _CCOL, _BCOL = build_consts()


OFF_QG, OFF_KG, OFF_VG, OFF_ZG = 0, 512, 1024, 1536
OFF_QM, OFF_KM, OFF_VM, OFF_OM = 2056, 2312, 2568, 3080
OFF_GG, OFF_GM = 3600, 4624

PV_NMIX, PV_NMLP, PV_CONVW, PV_GNORM, PV_MLNORM, PV_GBIAS, PV_GSIGN, PV_ALOG = 0, 8, 16, 64, 65, 69, 85, 101
PV_L = 105
PV_NFIN = 2 * PV_L
PV_COLS = PV_NFIN + 8


class B:
    pass


DEBUG_DUMP = bool(int(os.environ.get("MK_DEBUG", "0")))


def build_program(stage=99):
    nc = bass.Bass("TRN2", target_bir_lowering=False)
    S = Sched(nc)
    g = B()
    g.nc, g.S = nc, S

    def din(name, shape, dt=F32):
        return nc.dram_tensor(name, list(shape), dt, kind="ExternalInput").ap()

    def dout(name, shape, dt=F32):
        return nc.dram_tensor(name, list(shape), dt, kind="ExternalOutput").ap()

    g.xin = din("xin", [T, D])
    g.w_in = din("w_in", [DEPTH, D, IN_DIM])
    g.wg = din("wg", [DEPTH, D, 16])
    g.w_bg = din("w_bg", [DEPTH, 512, D])
    g.w_bm = din("w_bm", [DEPTH, 512, D])
    g.w_out = din("w_out", [DEPTH, D, D])
    g.w_up = din("w_up", [DEPTH, D, DFF])
    g.w_down = din("w_down", [DEPTH, DFF, D])
    g.pv = din("pv", [128, PV_COLS])
    g.cA = din("cA", list(_CA.shape))
    g.cB = din("cB", list(_CB.shape))
    g.cC = din("cC", list(_CC.shape))
    g.conv0 = din("conv0", [DEPTH, NSS, 3, 1536])
    g.S0 = din("S0", [DEPTH, NSS, 4, 128, 128])
    g.C0 = din("C0", [DEPTH, NSS, 4, 64, 128])
    g.n0 = din("n0", [DEPTH, NSS, 4, 64])
    g.m0 = din("m0", [DEPTH, NSS, 4])
    g.y = dout("y", [T, D])
    g.convp = dout("convp", [DEPTH, 3, 1536])
    g.Sp = dout("Sp", [DEPTH, 4, 128, 128])
    g.Cp = dout("Cp", [DEPTH, 4, 64, 128])
    g.np_ = dout("np_", [DEPTH, 4, 64])
    g.mp = dout("mp", [DEPTH, 4])
    g.convs = dout("convs", [DEPTH, NSS, 3, 1536])
    g.Ss = dout("Ss", [DEPTH, NSS, 4, 128, 128])
    g.Cs = dout("Cs", [DEPTH, NSS, 4, 64, 128])
    g.ns = dout("ns", [DEPTH, NSS, 4, 64])
    g.ms = dout("ms", [DEPTH, NSS, 4])

    A = Arena(nc, 16640, 229344)
    g.A = A
    g.banks = [nc.alloc_psum_tensor("psum%d" % i, [128, 512], F32) for i in range(6)]
    g.psb6 = nc.alloc_psum_tensor("psum6b", [128, 1024], BF16)
    g.banks.append(None)
    g.banks.append(nc.alloc_psum_tensor("psum7", [128, 512], F32))

    def bank(b):
        return g.banks[b]
    g.bank = bank

    g.xT = A.alloc([128, KC, T], F32, "xT")
    g.mixT = A.alloc([128, KC, T], BF16, "mixT")
    g.cAt = A.alloc([128, _CA.shape[1]], F32, "cA")
    g.pvt = A.alloc([128, PV_COLS], F32, "pv")
    g.identb = A.alloc([128, 128], BF16, "identb")
    g.onesb = A.alloc([128, 128], BF16, "onesb")
    g.smf = A.alloc([128, NSS, 64], BF16, "smf")
    g.epsb = A.alloc([128, 1], F32, "epsb")
    g.oneb = A.alloc([128, 1], F32, "oneb")

    def cst(name, rows=128):
        o, n = _CCOL[name]
        return g.cAt[0:rows, o:o + n]
    g.cst = cst
    g.ident = cst("ident")

    S.op("sp", DMA(g.cAt[:], g.cA), w=["cA"], dma=True)
    S.op("sp", DMA(g.pvt[:], g.pv), w=["pv"], dma=True)
    S.op("pool", DMA(g.smf[:].rearrange("p s f -> p (s f)"), g.cC), w=["smf"], dma=True)
    S.op("dve", CP(g.identb[:], g.ident), r=["cA"], w=["identb"])
    S.op("dve", MSET(g.onesb[:], 1.0), w=["onesb"])
    S.op("dve", MSET(g.epsb[:], EPS), w=["epsb"])
    S.op("dve", MSET(g.oneb[:], 1.0), w=["oneb"])
    S.barrier()

    pass0_load(g)
    for l in range(DEPTH):
        if stage >= 2:
            pass_gdn(g, l)
        if stage >= 3:
            pass_mlstm(g, l)
        if DEBUG_DUMP and l == 0:
            dbg = nc.dram_tensor("dbg_mix", [128, KC, T], BF16, kind="ExternalOutput").ap()
            S.op("sp", DMA(dbg, g.mixT[:]), dma=True)
            S.barrier()
        if stage >= 1:
            pass_mixout(g, l, with_mixer=(stage >= 2))
            pass_mlp(g, l)
    pass_final(g)
    S.emit()
    g.peak = A.peak
    return nc


def wview(w2d, kc):
    return w2d.rearrange("(c p) n -> p c n", p=128)


def pass0_load(g):
    S, A = g.S, g.A
    m = A.mark()
    xt = [A.alloc([128, D], F32, "xtok") for _ in range(2)]
    ntt = (T + 127) // 128
    for i in range(ntt):
        t0 = i * 128
        n = min(128, T - t0)
        buf = xt[i % 2]
        S.op("sp", DMA(buf[0:n, :], g.xin[t0:t0 + n, :]), w=[("xtok", i % 2)], dma=True)
        for c in range(KC):
            bk = c % 2
            S.op("pe", TR(g.bank(bk)[:, 0:n], buf[0:n, c * 128:(c + 1) * 128], g.ident[0:n, 0:n]),
                 r=[("xtok", i % 2)], w=[("ps", bk)])
            eng = "act" if c % 2 == 0 else "dve"
            if eng == "act":
                S.op("act", ACT(g.xT[:, c, t0:t0 + n], g.bank(bk)[:, 0:n], AF.Copy), r=[("ps", bk)], w=[("xT0", c)])
            else:
                S.op("dve", CP(g.xT[:, c, t0:t0 + n], g.bank(bk)[:, 0:n]), r=[("ps", bk)], w=[("xT0", c)])
    S.barrier()
    A.release(m)


def rmsnorm_tile(g, hT, hoff, t0, n, wcol, sqbufs, rstd, psb, tag, sqtok=None):
    S = g.S
    pb = g.bank(psb)
    btag = tag[0] if isinstance(tag, tuple) else tag
    if sqtok is None:
        sqtok = lambda i: ("sq", btag, i)
    for c in range(KC):
        sq = sqbufs[c % 2]
        S.op("act", ACT(sq[:, 0:n], g.xT[:, c, t0:t0 + n], AF.Square), r=["xT"], w=[sqtok(c % 2)])
        S.op("pe", MM(pb[:, 0:n], g.onesb[:, :], sq[:, 0:n], start=(c == 0), stop=(c == KC - 1)),
             r=[sqtok(c % 2)], w=[("ps", psb)])
    S.op("act", ACT(rstd[:, 0:n], pb[:, 0:n], AF.Ln, bias=g.epsb[:, 0:1], scale=1.0 / D), r=[("ps", psb)], w=[("rstd", btag)])
    S.op("act", ACT(rstd[:, 0:n], rstd[:, 0:n], AF.Exp, scale=-0.5), r=[("rstd", btag)], w=[("rstd", btag)])
    for c in range(KC):
        S.op("dve", STT(hT[:, c, hoff:hoff + n], g.xT[:, c, t0:t0 + n], g.pvt[:, wcol + c:wcol + c + 1], rstd[:, 0:n],
                        ALU.mult, ALU.mult), r=["xT", ("rstd", btag)], w=[("hT", tag)])


def wload(g, dst_ap, src_ap, tok):
    g.S.op("pool", DMA(dst_ap, src_ap), w=[tok], dma=True)


def pass_mlp(g, l):
    S, A = g.S, g.A
    m = A.mark()
    hT = A.alloc([128, KC, T], BF16, "hT")
    aT = g.mixT
    sqb = [A.alloc([128, 512], BF16, "sq") for _ in range(2)]
    rstd = A.alloc([128, 512], F32, "rstd")
    Wu = [A.alloc([128, KC, 512], BF16, "Wu") for _ in range(2)]
    Wd = [A.alloc([128, KC, 512], BF16, "Wd") for _ in range(2)]
    r32 = [A.alloc([128, 512], F32, "r32") for _ in range(2)]
    wup = wview(g.w_up[l], KC)
    nu = 0
    nd = 0
    for ti, (t0, n) in enumerate(TILES):
        rmsnorm_tile(g, hT, t0, t0, n, PV_L * l + PV_NMLP, sqb, rstd, 7, ("mlp", ti))
    bk = 0
    for grp in range(4):
        for half in range(2):
            slot = nu % 2
            nu += 1
            c0 = grp * 1024 + half * 512
            wload(g, Wu[slot][:], wup[:, :, c0:c0 + 512], ("Wu", slot))
            for j in range(4):
                jj = half * 4 + j
                for ti, (t0, n) in enumerate(TILES):
                    b = bk % 2
                    bk += 1
                    for k in range(KC):
                        S.op("pe", MM(g.bank(b)[:, 0:n], Wu[slot][:, k, j * 128:(j + 1) * 128], hT[:, k, t0:t0 + n],
                                      start=(k == 0), stop=(k == KC - 1)),
                             r=[("Wu", slot), ("hT", ("mlp", ti))], w=[("ps", b)])
                    S.op("act", ACT(r32[b][:, 0:n], g.bank(b)[:, 0:n], AF.Relu), r=[("ps", b)], w=[("r32", b)])
                    S.op("pool", TT(aT[:, jj, t0:t0 + n], r32[b][:, 0:n], r32[b][:, 0:n], ALU.mult),
                         r=[("r32", b)], w=[("aT", jj, ti)])
        wdn = g.w_down[l][grp * 1024:(grp + 1) * 1024, :].rearrange("(c p) n -> p c n", p=128)
        for half in range(2):
            slot = nd % 2
            nd += 1
            wload(g, Wd[slot][:], wdn[:, :, half * 512:(half + 1) * 512], ("Wd", slot))
            for j in range(4):
                mm_ = half * 4 + j
                for ti, (t0, n) in enumerate(TILES):
                    b = bk % 2
                    bk += 1
                    for k in range(KC):
                        S.op("pe", MM(g.bank(b)[:, 0:n], Wd[slot][:, k, j * 128:(j + 1) * 128], aT[:, k, t0:t0 + n],
                                      start=(k == 0), stop=(k == KC - 1)),
                             r=[("Wd", slot), ("aT", k, ti)], w=[("ps", b)])
                    S.op("dve", TT(g.xT[:, mm_, t0:t0 + n], g.xT[:, mm_, t0:t0 + n], g.bank(b)[:, 0:n], ALU.add),
                         r=[("ps", b), "xT"], w=["xT"])
    S.barrier()
    A.release(m)


def pass_mixout(g, l, with_mixer=True):
    S, A = g.S, g.A
    m = A.mark()
    hT = A.alloc([128, KC, T], BF16, "hT")
    HALF = [[0, 1], [2, 3, 4]]
    mg = A.alloc([128, KC, 1088], BF16, "merged")
    sqb = [A.alloc([128, 512], BF16, "sq") for _ in range(2)]
    rstd = A.alloc([128, 512], F32, "rstd")
    W8 = [A.alloc([128, KC, 256], BF16, "W8") for _ in range(4)]
    W4 = [A.alloc([128, 4, 256], BF16, "W4") for _ in range(4)]
    tmp = [A.alloc([128, 512], F32, "tmp") for _ in range(4)]
    tb = [A.alloc([128, 512], BF16, "tb") for _ in range(2)]
    win = wview(g.w_in[l], KC)
    for ti, (t0, n) in enumerate(TILES):
        rmsnorm_tile(g, hT, t0, t0, n, PV_L * l + PV_NMIX, sqb, rstd, 7, ("mix", ti))
    n8 = [0]
    n4 = [0]

    def slot8():
        s = n8[0] % 4
        n8[0] += 1
        return s

    def slot4():
        s = n4[0] % 4
        n4[0] += 1
        return s
    bk = 0
    if with_mixer:
        for which, off, func in ((0, OFF_ZG, AF.Silu), (1, OFF_OM, AF.Sigmoid)):
            for pair in range(2):
                s8 = slot8()
                wload(g, W8[s8][:], win[:, :, off + pair * 256: off + pair * 256 + 256], ("W8", s8))
                for j in range(2):
                    c = which * 4 + pair * 2 + j
                    for ti, (t0, n) in enumerate(TILES):
                        b = bk % 2
                        bk += 1
                        for k in range(KC):
                            S.op("pe", MM(g.bank(b)[:, 0:n], W8[s8][:, k, j * 128:(j + 1) * 128], hT[:, k, t0:t0 + n],
                                          start=(k == 0), stop=(k == KC - 1)),
                                 r=[("W8", s8), ("hT", ("mix", ti))], w=[("ps", b)])
                        S.op("act", ACT(tb[b][:, 0:n], g.bank(b)[:, 0:n], func), r=[("ps", b)], w=[("tb", b)])
                        S.op("pool", TT(g.mixT[:, c, t0:t0 + n], g.mixT[:, c, t0:t0 + n], tb[b][:, 0:n], ALU.mult),
                             r=[("tb", b), ("mixT", c, ti)], w=[("mixT", c, ti)])
    wbg = g.w_bg[l].rearrange("(c p) n -> p c n", p=128)
    wbm = g.w_bm[l].rearrange("(c p) n -> p c n", p=128)
    wout = wview(g.w_out[l], KC)
    for hf, tis in enumerate(HALF):
        hbase = TILES[tis[0]][0]
        if with_mixer:
            for pair in range(4):
                sg8, sm8, sg4, sm4 = slot8(), slot8(), slot4(), slot4()
                c0 = pair * 256
                wload(g, W4[sg4][:], wbg[:, :, c0:c0 + 256], ("W4", sg4))
                wload(g, W8[sg8][:], win[:, :, OFF_GG + c0:OFF_GG + c0 + 256], ("W8", sg8))
                wload(g, W4[sm4][:], wbm[:, :, c0:c0 + 256], ("W4", sm4))
                wload(g, W8[sm8][:], win[:, :, OFF_GM + c0:OFF_GM + c0 + 256], ("W8", sm8))
                for j in range(2):
                    nn = pair * 2 + j
                    for ti in tis:
                        t0, n = TILES[ti]
                        for k in range(4):
                            S.op("pe", MM(g.bank(0)[:, 0:n], W4[sg4][:, k, j * 128:(j + 1) * 128], g.mixT[:, k, t0:t0 + n],
                                          start=(k == 0), stop=(k == 3)),
                                 r=[("W4", sg4), ("mixT", k, ti)], w=[("ps", 0)])
                        for k in range(KC):
                            S.op("pe", MM(g.bank(1)[:, 0:n], W8[sg8][:, k, j * 128:(j + 1) * 128], hT[:, k, t0:t0 + n],
                                          start=(k == 0), stop=(k == KC - 1)),
                                 r=[("W8", sg8), ("hT", ("mix", ti))], w=[("ps", 1)])
                        for k in range(4):
                            S.op("pe", MM(g.bank(2)[:, 0:n], W4[sm4][:, k, j * 128:(j + 1) * 128], g.mixT[:, 4 + k, t0:t0 + n],
                                          start=(k == 0), stop=(k == 3)),
                                 r=[("W4", sm4), ("mixT", 4 + k, ti)], w=[("ps", 2)])
                        for k in range(KC):
                            S.op("pe", MM(g.bank(3)[:, 0:n], W8[sm8][:, k, j * 128:(j + 1) * 128], hT[:, k, t0:t0 + n],
                                          start=(k == 0), stop=(k == KC - 1)),
                                 r=[("W8", sm8), ("hT", ("mix", ti))], w=[("ps", 3)])
                        S.op("act", ACT(tmp[0][:, 0:n], g.bank(1)[:, 0:n], AF.Sigmoid), r=[("ps", 1)], w=[("tmp", 0)])
                        S.op("act", ACT(tmp[1][:, 0:n], g.bank(3)[:, 0:n], AF.Sigmoid), r=[("ps", 3)], w=[("tmp", 1)])
                        S.op("dve", TT(tmp[2][:, 0:n], tmp[0][:, 0:n], g.bank(0)[:, 0:n], ALU.mult),
                             r=[("tmp", 0), ("ps", 0)], w=[("tmp", 2)])
                        S.op("dve", TT(tmp[3][:, 0:n], tmp[1][:, 0:n], g.bank(2)[:, 0:n], ALU.mult),
                             r=[("tmp", 1), ("ps", 2)], w=[("tmp", 3)])
                        S.op("pool", TT(mg[:, nn, t0 - hbase:t0 - hbase + n], tmp[2][:, 0:n], tmp[3][:, 0:n], ALU.add),
                             r=[("tmp", 2), ("tmp", 3)], w=[("mg", nn, ti)])
            for pair in range(4):
                s8 = slot8()
                wload(g, W8[s8][:], wout[:, :, pair * 256:(pair + 1) * 256], ("W8", s8))
                for j in range(2):
                    mm_ = pair * 2 + j
                    for ti in tis:
                        t0, n = TILES[ti]
                        b = 4 + (bk % 2)
                        bk += 1
                        for k in range(KC):
                            S.op("pe", MM(g.bank(b)[:, 0:n], W8[s8][:, k, j * 128:(j + 1) * 128],
                                          mg[:, k, t0 - hbase:t0 - hbase + n], start=(k == 0), stop=(k == KC - 1)),
                                 r=[("W8", s8), ("mg", k, ti)], w=[("ps", b)])
                        S.op("dve", TT(g.xT[:, mm_, t0:t0 + n], g.xT[:, mm_, t0:t0 + n], g.bank(b)[:, 0:n], ALU.add),
                             r=[("ps", b), "xT"], w=["xT"])
    S.barrier()
    A.release(m)


def pass_final(g):
    S, A = g.S, g.A
    m = A.mark()
    sqb = [A.alloc([128, 512], BF16, "sq") for _ in range(2)]
    rstd = A.alloc([128, 512], F32, "rstd")
    yT = A.alloc([128, KC, 512], F32, "yT")
    yt = [A.alloc([128, D], F32, "ytok") for _ in range(2)]
    nst = 0
    for ti, (t0, n) in enumerate(TILES):
        pb = g.bank(7)
        for c in range(KC):
            sq = sqb[c % 2]
            S.op("act", ACT(sq[:, 0:n], g.xT[:, c, t0:t0 + n], AF.Square), r=["xT"], w=[("sq", c % 2)])
            S.op("pe", MM(pb[:, 0:n], g.onesb[:, :], sq[:, 0:n], start=(c == 0), stop=(c == KC - 1)),
                 r=[("sq", c % 2)], w=[("ps", 7)])
        S.op("act", ACT(rstd[:, 0:n], pb[:, 0:n], AF.Ln, bias=g.epsb[:, 0:1], scale=1.0 / D), r=[("ps", 7)], w=["rstd"])
        S.op("act", ACT(rstd[:, 0:n], rstd[:, 0:n], AF.Exp, scale=-0.5), r=["rstd"], w=["rstd"])
        for c in range(KC):
            S.op("dve", STT(yT[:, c, 0:n], g.xT[:, c, t0:t0 + n], g.pvt[:, PV_NFIN + c:PV_NFIN + c + 1], rstd[:, 0:n],
                            ALU.mult, ALU.mult), r=["xT", "rstd"], w=["yT"])
        for s0 in range(0, n, 128):
            ns_ = min(128, n - s0)
            buf = yt[nst % 2]
            tok = ("ytok", nst % 2)
            nst += 1
            for c in range(KC):
                b = c % 2
                S.op("pe", TR(g.bank(b)[0:ns_, 0:128], yT[:, c, s0:s0 + ns_], g.ident[:, :]), r=["yT"], w=[("ps", b)])
                if c % 2 == 0:
                    S.op("act", ACT(buf[0:ns_, c * 128:(c + 1) * 128], g.bank(b)[0:ns_, 0:128], AF.Copy),
                         r=[("ps", b)], w=[tok])
                else:
                    S.op("dve", CP(buf[0:ns_, c * 128:(c + 1) * 128], g.bank(b)[0:ns_, 0:128]), r=[("ps", b)], w=[tok])
            S.op("sp", DMA(g.y[t0 + s0:t0 + s0 + ns_, :], buf[0:ns_, :]), r=[tok], dma=True)
    S.barrier()
    A.release(m)


def _fm(v, nch):
    return np.ascontiguousarray(np.asarray(v, np.float32).reshape(nch, 128).T)


def _build_pv(inp):
    pv = np.zeros((128, PV_COLS), np.float32)
    for l in range(DEPTH):
        o = PV_L * l
        pv[:, o + PV_NMIX:o + PV_NMIX + 8] = _fm(inp["norm_mix"][l], 8)
        pv[:, o + PV_NMLP:o + PV_NMLP + 8] = _fm(inp["norm_mlp"][l], 8)
        cw = np.asarray(inp["gdn_conv_w"][l], np.float32)
        pv[:, o + PV_CONVW:o + PV_CONVW + 48] = cw.reshape(4, 12, 128).transpose(2, 1, 0).reshape(128, 48)
        pv[:, o + PV_GNORM] = np.asarray(inp["gdn_norm"][l], np.float32)
        pv[:, o + PV_MLNORM:o + PV_MLNORM + 4] = _fm(inp["ml_norm"][l], 4)
        gb = np.zeros(16, np.float32)
        gb[4:8] = inp["gdn_dt_bias"][l]
        gb[8:12] = inp["ml_i_bias"][l]
        gb[12:16] = inp["ml_f_bias"][l]
        pv[:, o + PV_GBIAS:o + PV_GBIAS + 16] = gb[None, :]
        pv[:, o + PV_GSIGN:o + PV_GSIGN + 16] = np.array([-1] * 4 + [1] * 4 + [1] * 4 + [-1] * 4, np.float32)[None, :]
        pv[:, o + PV_ALOG:o + PV_ALOG + 4] = np.asarray(inp["gdn_A_log"][l], np.float32)[None, :]
    pv[:, PV_NFIN:PV_NFIN + 8] = _fm(inp["norm_final"], 8)
    return pv


_PROG = {}


def _get_prog(stage):
    if stage not in _PROG:
        _PROG[stage] = build_program(stage)
    return _PROG[stage]


def kernel(**inputs):
    stage = int(os.environ.get("MK_STAGE", "99"))
    inp = {k: np.asarray(v) for k, v in inputs.items()}
    f32 = lambda a: np.ascontiguousarray(a, dtype=np.float32)
    w_in = f32(inp["w_in"])
    wg = np.ascontiguousarray(np.concatenate([w_in[:, :, 2048:2056], w_in[:, :, 3592:3600]], axis=2))
    shared = {
        "w_in": w_in, "wg": wg, "w_bg": f32(inp["w_branch_gdn"]), "w_bm": f32(inp["w_branch_ml"]),
        "w_out": f32(inp["w_out"]), "w_up": f32(inp["w_up"]), "w_down": f32(inp["w_down"]),
        "pv": _build_pv(inp), "cA": _CA, "cB": _CB, "cC": _CC,
    }
    xp, xs = f32(inp["x_prompt"]), f32(inp["x_sample"])
    in_maps = []
    for c in range(NCORES):
        sl = slice(c * NSS, (c + 1) * NSS)
        m = dict(shared)
        m["xin"] = np.ascontiguousarray(np.concatenate([xp[c], xs[sl].reshape(NSS * TS_, D)], axis=0))
        m["conv0"] = f32(inp["state_gdn_conv"][:, sl])
        m["S0"] = f32(inp["state_gdn_S"][:, sl])
        m["C0"] = f32(inp["state_mlstm_C"][:, sl])
        m["n0"] = f32(inp["state_mlstm_n"][:, sl])
        m["m0"] = f32(inp["state_mlstm_m"][:, sl])
        in_maps.append(m)
    nc = _get_prog(stage)
    res = run_bass_kernel_spmd(nc, in_maps, core_ids=list(range(NCORES)))
    R = res.results
    y = np.stack([r["y"] for r in R])
    y_prompt = np.ascontiguousarray(y[:, :SEQ])
    y_sample = np.ascontiguousarray(y[:, SEQ:].reshape(NCORES * NSS, TS_, D))
    cat1 = lambda k: np.ascontiguousarray(np.stack([r[k] for r in R], axis=1))
    cats = lambda k: np.ascontiguousarray(np.concatenate([r[k] for r in R], axis=1))
    return (y_prompt, y_sample, cat1("convp"), cat1("Sp"), cat1("Cp"), cat1("np_"), cat1("mp"),
            cats("convs"), cats("Ss"), cats("Cs"), cats("ns"), cats("ms"))


NDT = F32
NL_P, NL_S = 5, 1
LN_QSCALE = float(np.log(128.0 ** -0.5))


def pass_gdn(g, l):
    S, A = g.S, g.A
    m = A.mark()
    pvo = PV_L * l
    cst = g.cst
    ident64 = cst("ident", 64)[:, 0:64]
    L2 = cst("L2")
    ones64 = cst("ones", 64)
    reg0 = A.mark()
    Wq = A.alloc([128, KC, 1536], BF16, "Wq")
    Wg = A.alloc([128, KC, 16], BF16, "Wg")
    win = wview(g.w_in[l], KC)
    for i in range(3):
        wload(g, Wq[:, :, i * 512:(i + 1) * 512], win[:, :, i * 512:(i + 1) * 512], ("Wq", i))
    wload(g, Wg[:], wview(g.wg[l], KC), "Wg")
    cdiag = A.alloc([128, 48, 128], BF16, "cdiag")
    for cj in range(48):
        S.op("pool", TSC(cdiag[:, cj, :], g.identb[:, :], g.pvt[:, pvo + PV_CONVW + cj:pvo + PV_CONVW + cj + 1], ALU.mult),
             w=[("cdiag", cj)])
    negA = A.alloc([64, 4], F32, "negA")
    S.op("act", ACT(negA[:], g.pvt[0:64, pvo + PV_ALOG:pvo + PV_ALOG + 4], AF.Exp), w=["negA"])
    S.op("dve", TSC(negA[:], negA[:], -1.0, ALU.mult), r=["negA"], w=["negA"])
    hT = A.alloc([128, KC, 512], BF16, "hT")
    rstd = A.alloc([128, 512], F32, "rstd")
    pcc = [A.alloc([128, 3 + 512], BF16, "pcc") for _ in range(2)]
    sqb = pcc
    reg1 = A.mark()
    hist = A.alloc([128, 12, 3], BF16, "hist")
    cst32 = A.alloc([128, 12, 48], F32, "cst32")
    qkvT = A.alloc([128, 12, 512], BF16, "qkvT")
    sqk = A.alloc([128, 8, 64], BF16, "sqk")
    cio = A.alloc([48, 768], F32, "cio")
    hists = A.alloc([128, 12, 48], BF16, "hists")
    G1 = A.alloc([64, 8, 8], F32, "G1")
    Lg = A.alloc([64, 8, 8], F32, "Lg")
    gv = A.alloc([64, 8, 4], F32, "gv")
    gcs = A.alloc([64, 8, 4], F32, "gcs")
    gtots = A.alloc([64, 8, 4], F32, "gtots")
    LN = A.alloc([64, 8, 8], F32, "LN")
    VEC = A.alloc([64, 8, 5, 4], F32, "VEC")
    tmpv = A.alloc([64, 8, 4], F32, "tmpv")
    SC = A.alloc([64, 8, 4, 4], F32, "SC")
    BV = A.alloc([64, 8, 3, 4], F32, "BV")
    rdl = A.alloc([64, 64], F32, "rdl")
    dl = A.alloc([128, 64], F32, "dl")
    RH = A.alloc([128, 768], F32, "RH")
    X = A.alloc([64, 768], F32, "X")
    DCQ = [A.alloc([64, 3, 4, 64], NDT, "DCQ") for _ in range(2)]
    TTb = A.alloc([64, 4, 64], BF16, "TTb")
    QKd2 = [A.alloc([64, 4, 64], BF16, "QKd") for _ in range(2)]
    kvtm = A.alloc([64, 8, 128], BF16, "kvtm")
    kbg = A.alloc([64, 4, 128], BF16, "kbg")
    vb = A.alloc([64, 4, 128], BF16, "vb")
    kd2 = [A.alloc([64, 4, 128], BF16, "kd") for _ in range(2)]
    wTs2 = [A.alloc([128, 4, 64], BF16, "wTs")] * 2
    us2 = [A.alloc([64, 4, 128], F32, "us")] * 2
    junk = A.alloc([64, 128], F32, "junk")
    vn = A.alloc([64, 4, 128], BF16, "vn")
    o32 = A.alloc([64, 4, 128], F32, "o32")
    on = A.alloc([64, 4, 128], BF16, "on")
    sso = A.alloc([64, 4], F32, "sso")
    S32 = A.alloc([128, 4, 128], F32, "S32")
    Sb = A.alloc([128, 4, 128], BF16, "Sb")
    S.op("dve", MSET(S32[:], 0.0), w=["S32"])
    S.op("dve", MSET(Sb[:], 0.0), w=["Sb"])
    S.op("pool", MSET(hist[:], 0.0), w=["hist"])
    b6 = g.psb6
    gchunk = 0

    for ti, (t0, n) in enumerate(TILES):
        sample = (ti == 4)
        nch = n // 64
        md = "s" if sample else "p"
        NS = NSS if sample else 1
        NL = NL_S if sample else NL_P
        rhm = "RH_m"
        if ti == 0 or sample:
            o_, n_ = _BCOL["G" + md]
            S.op("sp", DMA(RH[64:128, :], g.cB[:, o_:o_ + n_]), w=["RH_m"], dma=True)
        rmsnorm_tile(g, hT, 0, t0, n, pvo + PV_NMIX, sqb, rstd, 7, "g", sqtok=lambda i: ("pcc", i))
        for ch in range(nch):
            for k in range(KC):
                S.op("pe", MM(g.bank(2)[0:64, ch * 8:(ch + 1) * 8], hT[:, k, ch * 64:(ch + 1) * 64], Wg[:, k, 0:8],
                              start=(k == 0), stop=(k == KC - 1)), r=[("hT", "g"), "Wg"], w=[("ps", 2)])
        gps = g.bank(2)[0:64, 0:nch * 8].rearrange("p (c x) -> p c x", x=8)
        S.op("dve", TT(G1[:, 0:nch, :], gps, bc(g.pvt[0:64, pvo + PV_GBIAS:pvo + PV_GBIAS + 8], [64, nch, 8], 1), ALU.add),
             r=[("ps", 2)], w=["G1"])
        S.op("dve", TT(G1[:, 0:nch, :], G1[:, 0:nch, :], bc(g.pvt[0:64, pvo + PV_GSIGN:pvo + PV_GSIGN + 8], [64, nch, 8], 1),
                       ALU.mult), r=["G1"], w=["G1"])
        S.op("act", ACT(Lg[:, 0:nch, :], G1[:, 0:nch, :], AF.Exp), r=["G1"], w=["Lg"])
        S.op("act", ACT(Lg[:, 0:nch, :], Lg[:, 0:nch, :], AF.Ln, bias=g.oneb[0:64, 0:1]), r=["Lg"], w=["Lg"])
        S.op("dve", TSC(VEC[:, 0:nch, 3, :], Lg[:, 0:nch, 0:4], -1.0, ALU.mult), r=["Lg"], w=[("VEC", 3)])
        S.op("dve", TT(gv[:, 0:nch, :], Lg[:, 0:nch, 4:8], bc(negA[:, :], [64, nch, 4], 1), ALU.mult),
             r=["Lg", "negA"], w=["gv"])
        gvf = gv[:, 0:nch, :]
        S.op("pe", MM(g.bank(2)[0:64, 128:128 + nch * 4], cst("CUM" + md, 64), gvf), r=["gv"], w=[("ps", 2)])
        S.op("pe", MM(g.bank(2)[0:64, 192:192 + nch * 4], cst("TOT" + md, 64), gvf), r=["gv"], w=[("ps", 2)])
        S.op("act", ACT(gcs[:, 0:nch, :], g.bank(2)[0:64, 128:128 + nch * 4].rearrange("p (c x) -> p c x", x=4), AF.Copy),
             r=[("ps", 2)], w=["gcs"])
        S.op("act", ACT(gtots[:, 0:nch, :], g.bank(2)[0:64, 192:192 + nch * 4].rearrange("p (c x) -> p c x", x=4), AF.Copy),
             r=[("ps", 2)], w=["gtots"])
        if not sample:
            S.op("dve", TSC(rdl[:, 0:nch * 4], gcs[:, 0:nch, :], cst("LASTOHp", 64)[:, 0:1], ALU.mult), r=["gcs"], w=["rdl"])
        else:
            S.op("dve", TT(rdl[:, :].rearrange("p (s h) -> p s h", h=4), bc(gcs[:, 0, :], [64, NSS, 4], 1),
                           bc(cst("LASTOHs", 64), [64, NSS, 4], 2), ALU.mult), r=["gcs"], w=["rdl"])
        ndl = nch * NS * 4
        S.op("pe", MM(g.bank(2)[:, 256:256 + ndl], ones64, rdl[:, 0:ndl]), r=["rdl"], w=[("ps", 2)])
        S.op("act", ACT(dl[:, 0:ndl], g.bank(2)[:, 256:256 + ndl], AF.Exp), r=[("ps", 2)], w=["dl"])
        if sample:
            for c in range(12):
                bk = c % 2
                if c % 6 == 0:
                    S.op("sp", DMA(cio[:], g.conv0[l].rearrange("s j c -> (s j) c")[:, c * 128:c * 128 + 768]), w=["cio"], dma=True)
                S.op("pe", TR(g.bank(bk)[:, 0:48], cio[:, (c % 6) * 128:(c % 6 + 1) * 128], g.ident[0:48, 0:48]),
                     r=["cio"], w=[("ps", bk)])
                S.op("act", ACT(hists[:, c, :], g.bank(bk)[:, 0:48], AF.Copy), r=[("ps", bk)], w=[("hists", c)])
        for c in range(12):
            bk = c % 2
            pb = g.bank(bk)
            cb = g.bank(4 + bk)
            for k in range(KC):
                S.op("pe", MM(pb[:, 0:n], Wq[:, k, c * 128:(c + 1) * 128], hT[:, k, 0:n], start=(k == 0), stop=(k == KC - 1)),
                     r=[("Wq", c // 4), ("hT", "g")], w=[("ps", bk)])
            pc = pcc[bk]
            if not sample:
                S.op("pool", CP(pc[:, 0:3], hist[:, c, :]), r=["hist"], w=[("pcc", bk)])
                S.op("act", ACT(pc[:, 3:3 + n], pb[:, 0:n], AF.Copy), r=[("ps", bk)], w=[("pcc", bk)])
                if ti == 3:
                    S.op("dve", CP(cst32[:, c, 0:3], pb[:, n - 3:n]), r=[("ps", bk)], w=[("cst32", c)])
                for j in range(4):
                    S.op("pe", MM(cb[:, 0:n], cdiag[:, c * 4 + j, :], pc[:, j:j + n], start=(j == 0), stop=(j == 3)),
                         r=[("pcc", bk), ("cdiag", c * 4 + j)], w=[("ps", 4 + bk)])
                S.op("pool", CP(hist[:, c, :], pc[:, n:n + 3]), r=[("pcc", bk)], w=["hist"])
            else:
                pcs = pc[:, 0:NSS * 7].rearrange("p (s t) -> p s t", t=7)
                S.op("pool", CP(pcs[:, :, 0:3], hists[:, c, :].rearrange("p (s t) -> p s t", t=3)),
                     r=[("hists", c)], w=[("pcc", bk)])
                pbs = pb[:, 0:64].rearrange("p (s t) -> p s t", t=4)
                S.op("act", ACT(pcs[:, :, 3:7], pbs, AF.Copy), r=[("ps", bk)], w=[("pcc", bk)])
                S.op("dve", CP(cst32[:, c, :].rearrange("p (s t) -> p s t", t=3), pbs[:, :, 1:4]),
                     r=[("ps", bk)], w=[("cst32", c)])
                for j in range(4):
                    S.op("pe", MM(cb[:, 0:64], cdiag[:, c * 4 + j, :], pcs[:, :, j:j + 4],
                                  start=(j == 0), stop=(j == 3)),
                         r=[("pcc", bk), ("cdiag", c * 4 + j)], w=[("ps", 4 + bk)])
            S.op("act", ACT(qkvT[:, c, 0:n], cb[:, 0:n], AF.Silu), r=[("ps", 4 + bk)], w=[("qkvT", c)])
        if ti == 3 or sample:
            ncol = 48 if sample else 3
            cso = cio
            for c in range(12):
                bk = c % 2
                S.op("pe", TR(g.bank(bk)[0:ncol, 0:128], cst32[:, c, 0:ncol], g.ident[:, :]), r=[("cst32", c)], w=[("ps", bk)])
                S.op("act", ACT(cso[0:ncol, (c % 6) * 128:(c % 6 + 1) * 128], g.bank(bk)[0:ncol, 0:128], AF.Copy),
                     r=[("ps", bk)], w=["cio"])
                if c % 6 == 5:
                    c0_ = (c - 5) * 128
                    if sample:
                        S.op("sp", DMA(g.convs[l].rearrange("s j c -> (s j) c")[:, c0_:c0_ + 768], cso[0:48, :]), r=["cio"], dma=True)
                    else:
                        S.op("sp", DMA(g.convp[l][:, c0_:c0_ + 768], cso[0:3, :]), r=["cio"], dma=True)
        for ch in range(nch):
            S.op("act", ACT(sqk[:, :, :], qkvT[:, 0:8, ch * 64:(ch + 1) * 64], AF.Square), r=[("qkvT", c) for c in range(8)], w=["sqk"])
            for idx in range(8):
                S.op("pe", MM(g.bank(2)[0:64, 64 + ch * 8 + idx:64 + ch * 8 + idx + 1], sqk[:, idx, :],
                              g.onesb[:, 0:1]), r=["sqk"], w=[("ps", 2)])
        S.op("act", ACT(LN[:, 0:nch, :], g.bank(2)[0:64, 64:64 + nch * 8].rearrange("p (c x) -> p c x", x=8), AF.Ln,
                        bias=g.epsb[0:64, 0:1]), r=[("ps", 2)], w=["LN"])
        V = lambda s_: VEC[:, 0:nch, s_, :]
        S.op("dve", STT(V(0), LN[:, 0:nch, 4:8], -0.5, gcs[:, 0:nch, :], ALU.mult, ALU.subtract),
             r=["LN", "gcs"], w=[("VEC", 0)])
        S.op("dve", TT(tmpv[:, 0:nch, :], gcs[:, 0:nch, :], V(3), ALU.add), r=["gcs", ("VEC", 3)], w=["tmpv"])
        S.op("dve", STT(V(1), LN[:, 0:nch, 4:8], -0.5, tmpv[:, 0:nch, :], ALU.mult, ALU.add), r=["LN", "tmpv"], w=[("VEC", 1)])
        S.op("dve", STT(V(2), LN[:, 0:nch, 0:4], -0.5, gcs[:, 0:nch, :], ALU.mult, ALU.add), r=["LN", "gcs"], w=[("VEC", 2)])
        S.op("dve", TSC(V(2), V(2), LN_QSCALE, ALU.add), r=[("VEC", 2)], w=[("VEC", 2)])
        S.op("dve", TT(V(4), gtots[:, 0:nch, :], V(0), ALU.add), r=["gtots", ("VEC", 0)], w=[("VEC", 4)])
        S.op("act", ACT(SC[:, 0:nch, :, :], VEC[:, 0:nch, 1:5, :], AF.Exp), r=[("VEC", i) for i in range(1, 5)], w=["SC"])
        S.op("pool", CP(BV[:, 0:nch, 0, :], V(1)), r=[("VEC", 1)], w=["BV"])
        S.op("pool", CP(BV[:, 0:nch, 1, :], V(0)), r=[("VEC", 0)], w=["BV"])
        S.op("pool", CP(BV[:, 0:nch, 2, :], V(0)), r=[("VEC", 0)], w=["BV"])
        vecr = [("VEC", i) for i in range(5)]

        if sample:
            S.barrier()
            A2 = Arena(g.nc, reg0, reg1)
            qTm = [A2.alloc([128, NSS, 64], BF16, "qTm") for _ in range(2)]
            wTm = [A2.alloc([128, NSS, 64], BF16, "wTm") for _ in range(2)]
            kdm = [A2.alloc([64, NSS, 128], BF16, "kdm") for _ in range(2)]
            S0h = [A2.alloc([128, NSS, 128], F32, "S0h") for _ in range(2)]
            Sbh = [A2.alloc([128, NSS, 128], BF16, "Sbh") for _ in range(2)]
            smf = g.smf[:, :, :]

        def prep(ch, gch):
            par = gch % 2
            q0 = ch * 64
            QKd_, kd_, wTs_, us_ = QKd2[par], kd2[par], wTs2[par], us2[par]
            tQ, tkd, twT, tus = ("QKd", par), ("kd", par), "wTs", "us"
            for rnd in range(2):
                for i4 in range(4):
                    idx = rnd * 4 + i4
                    S.op("pe", TR(b6[0:64, 512 + i4 * 128:512 + (i4 + 1) * 128], qkvT[:, 4 + idx, q0:q0 + 64], g.identb[:, :]),
                         r=[("qkvT", 4 + idx)], w=[("ps", "6b")])
                S.op("act", ACT(kvtm[:, rnd * 4:rnd * 4 + 4, :], b6[0:64, 512:1024].rearrange("p (a b) -> p a b", b=128), AF.Copy),
                     r=[("ps", "6b")], w=[("kvtm", rnd)])
                yield
            for h in range(4):
                kT = qkvT[:, 4 + h, q0:q0 + 64]
                S.op("pe", MM(g.bank(3)[0:64, h * 64:(h + 1) * 64], kT, kT), r=[("qkvT", 4 + h)], w=[("ps", 3)])
            for h in range(4):
                S.op("pe", MM(g.bank(3)[0:64, 256 + h * 64:256 + (h + 1) * 64], qkvT[:, 4 + h, q0:q0 + 64],
                              qkvT[:, h, q0:q0 + 64]), r=[("qkvT", 4 + h), ("qkvT", h)], w=[("ps", 3)])
            yield
            S.op("dve", TT(RH[0:64, :].rearrange("p (a f) -> p a f", f=64), bc(ident64, [64, 12, 64], 1),
                           bc(VEC[:, ch, 0:3, :].rearrange("p a b -> p (a b)"), [64, 12, 64], 2), ALU.mult),
                 r=vecr[0:3], w=["RH_d"])
            S.op("pe", MM(g.bank(4)[0:64, 0:512], L2, RH[:, 0:512]), r=["RH_d", rhm], w=[("ps", 4)])
            S.op("pe", MM(g.bank(5)[0:64, 0:256], L2, RH[:, 512:768]), r=["RH_d", rhm], w=[("ps", 5)])
            yield
            S.op("dve", TT(X[:, 0:512].rearrange("p (a f) -> p a f", f=64), g.bank(4)[0:64, 0:512].rearrange("p (a f) -> p a f", f=64),
                           bc(BV[:, ch, 0:2, :].rearrange("p a b -> p (a b)"), [64, 8, 64], 2), ALU.add),
                 r=[("ps", 4), "BV"], w=["X"])
            S.op("dve", TT(X[:, 512:768].rearrange("p (a f) -> p a f", f=64), g.bank(5)[0:64, 0:256].rearrange("p (a f) -> p a f", f=64),
                           bc(BV[:, ch, 2, :], [64, 4, 64], 2), ALU.add), r=[("ps", 5), "BV"], w=["X"])
            S.op("act", ACT(X[:, :], X[:, :], AF.Exp), r=["X"], w=["X"])
            yield
            S.op("dve", STT(DCQ[0][:, 0:2, :, :].rearrange("p a h f -> p a (h f)"), X[:, 0:512].rearrange("p (a x) -> p a x", a=2),
                            -1.0, bc(g.bank(3)[0:64, 0:256], [64, 2, 256], 1), ALU.mult, ALU.mult),
                 r=["X", ("ps", 3)], w=[("DCQ", 0)])
            S.op("dve", TT(QKd_[:, :, :].rearrange("p h f -> p (h f)"), X[:, 512:768], g.bank(3)[0:64, 256:512], ALU.mult),
                 r=["X", ("ps", 3)], w=[tQ])
            S.op("dve", TT(DCQ[1][:, 2, :, :], bc(ident64, [64, 4, 64], 1), DCQ[0][:, 1, :, :], ALU.add),
                 r=[("DCQ", 0)], w=[("DCQ", 1)])
            yield
            S.op("pool", TT(kbg[:, :, :], kvtm[:, 0:4, :], bc(SC[:, ch, 0, :], [64, 4, 128], 2), ALU.mult), r=[("kvtm", 0), "SC"], w=["kbg"])
            S.op("pool", TT(vb[:, :, :], kvtm[:, 4:8, :], bc(SC[:, ch, 2, :], [64, 4, 128], 2), ALU.mult), r=[("kvtm", 1), "SC"], w=["vb"])
            S.op("pool", TT(kd_[:, :, :], kvtm[:, 0:4, :], bc(SC[:, ch, 3, :], [64, 4, 128], 2), ALU.mult), r=[("kvtm", 0), "SC"], w=[tkd])
            for s_ in range(1, NL + 2):
                rb, wb = DCQ[(s_ - 1) % 2], DCQ[s_ % 2]
                rt, wt = ("DCQ", (s_ - 1) % 2), ("DCQ", s_ % 2)
                for h in range(4):
                    if s_ <= NL:
                        S.op("pe", MM(g.bank(4)[0:64, h * 64:(h + 1) * 64], rb[:, 1, h, :], rb[:, 0, h, :]), r=[rt], w=[("ps", 4)])
                        S.op("pe", MM(g.bank(4)[0:64, 256 + h * 64:256 + (h + 1) * 64], rb[:, 0, h, :], rb[:, 1, h, :]), r=[rt], w=[("ps", 4)])
                    if s_ >= 2:
                        S.op("pe", MM(g.bank(5)[0:64, h * 64:(h + 1) * 64], rb[:, 0, h, :], rb[:, 2, h, :]), r=[rt], w=[("ps", 5)])
                if s_ <= NL:
                    S.op("act", ACT(wb[:, 0:2, :, :].rearrange("p a h f -> p (a h f)"), g.bank(4)[0:64, 0:512], AF.Copy),
                         r=[("ps", 4)], w=[wt])
                if s_ >= 2:
                    S.op("dve", TT(wb[:, 2, :, :].rearrange("p h f -> p (h f)"), rb[:, 2, :, :].rearrange("p h f -> p (h f)"),
                                   g.bank(5)[0:64, 0:256], ALU.add), r=[rt, ("ps", 5)], w=[wt])
                yield
            fin = DCQ[(NL + 1) % 2]
            S.op("act", ACT(TTb[:, :, :], fin[:, 2, :, :], AF.Copy), r=[("DCQ", (NL + 1) % 2)], w=["TTb"])
            for h in range(4):
                S.op("pe", MM(g.bank(7)[:, h * 64:(h + 1) * 64], kbg[:, h, :], TTb[:, h, :]), r=["kbg", "TTb"], w=[("ps", 7)])
            for h in range(4):
                S.op("pe", MM(g.bank(3)[0:64, h * 128:(h + 1) * 128], TTb[:, h, :], vb[:, h, :]), r=["vb", "TTb"], w=[("ps", 3)])
            yield
            S.op("act", ACT(wTs_[:, :, :], g.bank(7)[:, 0:256].rearrange("p (h f) -> p h f", h=4), AF.Copy), r=[("ps", 7)], w=[twT])
            S.op("act", ACT(us_[:, :, :], g.bank(3)[0:64, :].rearrange("p (h v) -> p h v", h=4), AF.Copy), r=[("ps", 3)], w=[tus])
            yield

        def rec(ch, gch):
            par = gch % 2
            q0 = ch * 64
            QKd_, kd_, wTs_, us_ = QKd2[par], kd2[par], wTs2[par], us2[par]
            tQ, tkd, twT, tus = ("QKd", par), ("kd", par), "wTs", "us"
            if not sample:
                S.op("dve", TT(S32[:, :, :], S32[:, :, :], bc(dl[:, ch * 4:(ch + 1) * 4], [128, 4, 128], 2), ALU.mult),
                     r=["S32", "dl"], w=["S32"])
                for h in range(4):
                    S.op("pe", MM(g.bank(0)[0:64, h * 128:(h + 1) * 128], wTs_[:, h, :], Sb[:, h, :]), r=[twT, "Sb"], w=[("ps", 0)])
                for h in range(4):
                    S.op("pe", MM(g.bank(1)[0:64, h * 128:(h + 1) * 128], qkvT[:, h, q0:q0 + 64], Sb[:, h, :]),
                         r=[("qkvT", h), "Sb"], w=[("ps", 1)])
                yield
                S.op("dve", TT(vn[:, :, :], us_[:, :, :], g.bank(0)[0:64, :].rearrange("p (h v) -> p h v", h=4), ALU.subtract),
                     r=[tus, ("ps", 0)], w=["vn"])
                yield
                for h in range(4):
                    S.op("pe", MM(g.bank(0)[:, h * 128:(h + 1) * 128], kd_[:, h, :], vn[:, h, :]), r=[tkd, "vn"], w=[("ps", 0)])
                for h in range(4):
                    S.op("pe", MM(g.bank(2)[0:64, h * 128:(h + 1) * 128], QKd_[:, h, :], vn[:, h, :]), r=[tQ, "vn"], w=[("ps", 2)])
                S.op("dve", TT(o32[:, :, :], g.bank(1)[0:64, :].rearrange("p (h v) -> p h v", h=4), bc(SC[:, ch, 1, :], [64, 4, 128], 2),
                               ALU.mult), r=[("ps", 1), "SC"], w=["o32"])
                yield
                S.op("dve", TT(Sb[:, :, :], S32[:, :, :], g.bank(0)[:, :].rearrange("p (h v) -> p h v", h=4), ALU.add),
                     r=["S32", ("ps", 0)], w=["Sb"])
                S.op("dve", TT(S32[:, :, :], S32[:, :, :], g.bank(0)[:, :].rearrange("p (h v) -> p h v", h=4), ALU.add),
                     r=["S32", ("ps", 0)], w=["S32"])
                yield
            else:
                for h in range(4):
                    sl = h % 2
                    S.op("pool", TT(qTm[sl][:, :, :], bc(qkvT[:, h, 0:64], [128, NSS, 64], 1), smf, ALU.mult),
                         r=[("qkvT", h)], w=[("qTm", sl)])
                    S.op("dve", TT(wTm[sl][:, :, :], bc(wTs_[:, h, :], [128, NSS, 64], 1), smf, ALU.mult),
                         r=[twT], w=[("wTm", sl)])
                    S.op("pool", TT(kdm[sl][:, :, :], bc(kd_[:, h, :], [64, NSS, 128], 1), bc(cst("SEQOHs", 64), [64, NSS, 128], 2),
                                    ALU.mult), r=[tkd], w=[("kdm", sl)])
                    src = g.S0[l, :, h, :, :].rearrange("s k v -> k s v")
                    S.op("sp", DMA(S0h[sl][:], src), w=[("S0h", sl)], dma=True)
                    wload(g, Sbh[sl][:], src, ("Sbh", sl))
                    for s_ in range(NSS):
                        S.op("pe", MM(g.bank(0)[0:64, h * 128:(h + 1) * 128], wTm[sl][:, s_, :], Sbh[sl][:, s_, :],
                                      start=(s_ == 0), stop=(s_ == NSS - 1)), r=[("wTm", sl), ("Sbh", sl)], w=[("ps", 0)])
                    S.op("dve", TT(vn[:, h, :], us_[:, h, :], g.bank(0)[0:64, h * 128:(h + 1) * 128], ALU.subtract),
                         r=[tus, ("ps", 0)], w=[("vn", h)])
                    for s_ in range(NSS):
                        S.op("pe", MM(g.bank(1)[0:64, h * 128:(h + 1) * 128], qTm[sl][:, s_, :], Sbh[sl][:, s_, :],
                                      start=(s_ == 0), stop=(s_ == NSS - 1)), r=[("qTm", sl), ("Sbh", sl)], w=[("ps", 1)])
                    S.op("pe", MM(g.bank(2)[0:64, h * 128:(h + 1) * 128], QKd_[:, h, :], vn[:, h, :]), r=[tQ, ("vn", h)], w=[("ps", 2)])
                    for grp in range(4):
                        pbk = 4 + (grp % 2)
                        for s4 in range(4):
                            s_ = grp * 4 + s4
                            S.op("pe", MM(g.bank(pbk)[:, s4 * 128:(s4 + 1) * 128], kdm[sl][:, s_, :], vn[:, h, :]),
                                 r=[("kdm", sl), ("vn", h)], w=[("ps", pbk)])
                        for s4 in range(4):
                            s_ = grp * 4 + s4
                            S.op("dve", STT(S0h[sl][:, s_, :], S0h[sl][:, s_, :], dl[:, s_ * 4 + h:s_ * 4 + h + 1],
                                            g.bank(pbk)[:, s4 * 128:(s4 + 1) * 128], ALU.mult, ALU.add),
                                 r=[("S0h", sl), "dl", ("ps", pbk)], w=[("S0h", sl)])
                    S.op("sp", DMA(g.Ss[l, :, h, :, :].rearrange("s k v -> k s v"), S0h[sl][:]), r=[("S0h", sl)], dma=True)
                S.op("dve", TT(o32[:, :, :], g.bank(1)[0:64, :].rearrange("p (h v) -> p h v", h=4), bc(SC[:, ch, 1, :], [64, 4, 128], 2),
                               ALU.mult), r=[("ps", 1), "SC"], w=["o32"])
            S.op("dve", TT(o32[:, :, :], o32[:, :, :], g.bank(2)[0:64, :].rearrange("p (h v) -> p h v", h=4), ALU.add),
                 r=["o32", ("ps", 2)], w=["o32"])
            for h in range(4):
                S.op("act", ACT(junk[:, :], o32[:, h, :], AF.Square, accum=sso[:, h:h + 1]), r=["o32"], w=["junk", ("sso", h)])
            yield
            ssr = [("sso", h) for h in range(4)]
            S.op("act", ACT(sso[:, :], sso[:, :], AF.Ln, bias=g.epsb[0:64, 0:1], scale=1.0 / 128), r=ssr, w=ssr)
            S.op("act", ACT(sso[:, :], sso[:, :], AF.Exp, scale=-0.5), r=ssr, w=ssr)
            S.op("pool", TT(on[:, :, :], o32[:, :, :], bc(sso[:, :], [64, 4, 128], 2), ALU.mult), r=["o32"] + ssr, w=["on"])
            yield
            for h in range(4):
                S.op("pe", TR(b6[:, h * 64:(h + 1) * 64], on[:, h, :], g.identb[0:64, 0:64]), r=["on"], w=[("ps", "6a")])
            S.op("act", ACT(g.mixT[:, 0:4, t0 + q0:t0 + q0 + 64], b6[:, 0:256].rearrange("p (h f) -> p h f", h=4), AF.Copy,
                            scale=g.pvt[:, pvo + PV_GNORM:pvo + PV_GNORM + 1]), r=[("ps", "6a")],
                 w=[("mixT", c_, ti) for c_ in range(4)])
            yield

        def drain(*gens):
            gens = list(gens)
            while gens:
                for gg in list(gens):
                    try:
                        next(gg)
                    except StopIteration:
                        gens.remove(gg)
        drain(prep(0, gchunk))
        for ch in range(nch):
            if ch + 1 < nch:
                drain(rec(ch, gchunk), prep(ch + 1, gchunk + 1))
            else:
                drain(rec(ch, gchunk))
            gchunk += 1
        if ti == 3:
            S.op("sp", DMA(g.Sp[l].rearrange("h k v -> k h v"), S32[:, :, :]), r=["S32"], dma=True)
    S.barrier()
    A.release(m)


def pass_mlstm(g, l):
    S, A = g.S, g.A
    m = A.mark()
    pvo = PV_L * l
    cst = g.cst
    ident64 = cst("ident", 64)[:, 0:64]
    L2 = cst("L2")
    ones64 = cst("ones", 64)
    reg0 = A.mark()
    Wm = A.alloc([128, KC, 1024], BF16, "Wm")
    Wg = A.alloc([128, KC, 16], BF16, "Wg")
    win = wview(g.w_in[l], KC)
    for i in range(2):
        wload(g, Wm[:, :, i * 512:(i + 1) * 512], win[:, :, OFF_QM + i * 512:OFF_QM + (i + 1) * 512], ("Wm", i))
    wload(g, Wg[:], wview(g.wg[l], KC), "Wg")
    hT = A.alloc([128, KC, 512], BF16, "hT")
    rstd = A.alloc([128, 512], F32, "rstd")
    sqb = [A.alloc([128, 512], BF16, "sq") for _ in range(2)]
    reg1 = A.mark()
    qT = A.alloc([64, 4, 512], BF16, "qT")
    kT = A.alloc([64, 4, 512], BF16, "kT")
    vtm = A.alloc([64, 8, 4, 128], BF16, "vtm")
    ktm = A.alloc([64, 8, 4, 64], BF16, "ktm")
    G1 = A.alloc([64, 8, 8], F32, "G1")
    Lf = A.alloc([64, 8, 4], F32, "Lf")
    Fc = A.alloc([64, 8, 4], F32, "Fc")
    d1 = A.alloc([64, 8, 4], F32, "d1")
    RH1 = A.alloc([128, 256], F32, "RH1")
    RH2 = A.alloc([128, 256], F32, "RH2")
    mx = A.alloc([64, 4], F32, "mx")
    Mx = A.alloc([64, 4], F32, "Mx")
    mm2 = A.alloc([64, 8], F32, "mm2")
    mprev = A.alloc([64, 4], F32, "mprev")
    MxL = A.alloc([64, 4], F32, "MxL")
    negMx = A.alloc([64, 4], F32, "negMx")
    av = A.alloc([64, 4], F32, "av")
    enm = A.alloc([64, 4], F32, "enm")
    wC = A.alloc([64, 4], F32, "wC")
    tv = A.alloc([64, 4], F32, "tv")
    rdec = A.alloc([64, 64], F32, "rdec")
    dec = A.alloc([64, 64], F32, "dec")
    X2 = A.alloc([64, 256], F32, "X2")
    SmT = A.alloc([64, 4, 64], BF16, "SmT")
    P2s = A.alloc([64, 4, 128], F32, "P2s")
    num = A.alloc([64, 4, 128], F32, "num")
    hn = A.alloc([64, 4, 128], BF16, "hn")
    dd = A.alloc([64, 4], F32, "dd")
    rden = A.alloc([64, 4], F32, "rden")
    ssn = A.alloc([64, 4], F32, "ssn")
    kw = A.alloc([64, 4, 64], BF16, "kw")
    C32 = A.alloc([64, 4, 128], F32, "C32")
    Cb = A.alloc([64, 4, 128], BF16, "Cb")
    n32 = A.alloc([64, 4], F32, "n32")
    nb = A.alloc([64, 4], BF16, "nb")
    mo = A.alloc([16, 4], F32, "mo")
    nio = A.alloc([64, 64], F32, "nio")
    S.op("dve", MSET(C32[:], 0.0), w=["C32"])
    S.op("dve", MSET(Cb[:], 0.0), w=["Cb"])
    S.op("dve", MSET(n32[:], 0.0), w=["n32"])
    S.op("dve", MSET(nb[:], 0.0), w=["nb"])
    S.op("dve", MSET(mprev[:], 0.0), w=["mprev"])
    b6 = g.psb6
    b2 = g.bank(2)

    for ti, (t0, n) in enumerate(TILES):
        sample = (ti == 4)
        nch = n // 64
        md = "s" if sample else "p"
        NS = NSS if sample else 1
        if ti == 0 or sample:
            o_, n_ = _BCOL["M1" + md]
            S.op("sp", DMA(RH1[64:128, :], g.cB[:, o_:o_ + n_]), w=["RH1_m"], dma=True)
            o_, n_ = _BCOL["M2" + md]
            S.op("sp", DMA(RH2[64:128, :], g.cB[:, o_:o_ + n_]), w=["RH2_m"], dma=True)
        rmsnorm_tile(g, hT, 0, t0, n, pvo + PV_NMIX, sqb, rstd, 7, "m")
        for ch in range(nch):
            for k in range(KC):
                S.op("pe", MM(b2[0:64, ch * 8:(ch + 1) * 8], hT[:, k, ch * 64:(ch + 1) * 64], Wg[:, k, 8:16],
                              start=(k == 0), stop=(k == KC - 1)), r=[("hT", "m"), "Wg"], w=[("ps", 2)])
        gps = b2[0:64, 0:nch * 8].rearrange("p (c x) -> p c x", x=8)
        S.op("dve", TT(G1[:, 0:nch, :], gps, bc(g.pvt[0:64, pvo + PV_GBIAS + 8:pvo + PV_GBIAS + 16], [64, nch, 8], 1), ALU.add),
             r=[("ps", 2)], w=["G1"])
        S.op("act", ACT(Lf[:, 0:nch, :], G1[:, 0:nch, 4:8], AF.Exp, scale=-1.0), r=["G1"], w=["Lf"])
        S.op("act", ACT(Lf[:, 0:nch, :], Lf[:, 0:nch, :], AF.Ln, bias=g.oneb[0:64, 0:1]), r=["Lf"], w=["Lf"])
        S.op("pe", MM(b2[0:64, 128:128 + nch * 4], cst("CUM" + md, 64), Lf[:, 0:nch, :]), r=["Lf"], w=[("ps", 2)])
        cps = b2[0:64, 128:128 + nch * 4].rearrange("p (c x) -> p c x", x=4)
        S.op("dve", TSC(Fc[:, 0:nch, :], cps, -1.0, ALU.mult), r=[("ps", 2)], w=["Fc"])
        S.op("dve", TT(d1[:, 0:nch, :], G1[:, 0:nch, 0:4], cps, ALU.add), r=["G1", ("ps", 2)], w=["d1"])
        for qk in range(2):
            dst = qT if qk == 0 else kT
            for h in range(4):
                bk = h % 2
                col = qk * 256 + h * 64
                for k in range(KC):
                    S.op("pe", MM(g.bank(bk)[0:64, 0:n], Wm[:, k, col:col + 64], hT[:, k, 0:n], start=(k == 0), stop=(k == KC - 1)),
                         r=[("Wm", 0), ("hT", "m")], w=[("ps", bk)])
                S.op("act", ACT(dst[:, h, 0:n], g.bank(bk)[0:64, 0:n], AF.Copy, scale=(0.125 if qk == 0 else 1.0)),
                     r=[("ps", bk)], w=[("qkT", qk, h)])
        for ch in range(nch):
            for k in range(KC):
                S.op("pe", MM(g.bank(0)[0:64, 0:512], hT[:, k, ch * 64:(ch + 1) * 64], Wm[:, k, 512:1024],
                              start=(k == 0), stop=(k == KC - 1)), r=[("Wm", 1), ("hT", "m")], w=[("ps", 0)])
            S.op("act", ACT(vtm[:, ch, :, :], g.bank(0)[0:64, 0:512].rearrange("p (h v) -> p h v", h=4), AF.Copy),
                 r=[("ps", 0)], w=[("vtm", ch)])
            for k in range(KC):
                S.op("pe", MM(g.bank(1)[0:64, 0:256], hT[:, k, ch * 64:(ch + 1) * 64], Wm[:, k, 256:512],
                              start=(k == 0), stop=(k == KC - 1)), r=[("Wm", 0), ("hT", "m")], w=[("ps", 1)])
            S.op("dve", CP(ktm[:, ch, :, :], g.bank(1)[0:64, 0:256].rearrange("p (h v) -> p h v", h=4)),
                 r=[("ps", 1)], w=[("ktm", ch)])
        if sample:
            A2 = A
            qTm = [A2.alloc([64, NSS, 64], BF16, "qTm") for _ in range(2)]
            kwm = [A2.alloc([64, NSS, 64], BF16, "kwm") for _ in range(2)]
            C0h = [A2.alloc([64, NSS, 128], F32, "C0h") for _ in range(2)]
            Cbh = [A2.alloc([64, NSS, 128], BF16, "Cbh") for _ in range(2)]
            n0t = A2.alloc([64, NSS, 4], F32, "n0t")
            n0b = A2.alloc([64, NSS, 4], BF16, "n0b")
            m0t = A2.alloc([16, 4], F32, "m0t")
            smf = g.smf[0:64, :, :]
            S.op("sp", DMA(m0t[:], g.m0[l]), w=["m0t"], dma=True)
            S.op("pe", MM(b2[0:64, 200:204], cst("EXPAND", 16), m0t[:, :]), r=["m0t"], w=[("ps", 2)])
            S.op("act", ACT(mprev[:, :], b2[0:64, 200:204], AF.Copy), r=[("ps", 2)], w=["mprev"])
            S.op("sp", DMA(nio[:], g.n0[l].rearrange("s h k -> (s h) k")), w=["nio"], dma=True)
            S.op("pe", TR(g.bank(3)[0:64, 0:64], nio[:, :], ident64), r=["nio"], w=[("ps", 3)])
            S.op("act", ACT(n0t[:, :, :].rearrange("p s h -> p (s h)"), g.bank(3)[0:64, 0:64], AF.Copy), r=[("ps", 3)], w=["n0t"])
            S.op("dve", CP(n0b[:, :, :], n0t[:, :, :]), r=["n0t"], w=["n0b"])

        for ch in range(nch):
            q0 = ch * 64
            S.op("dve", TT(RH1[0:64, :].rearrange("p (h f) -> p h f", f=64), bc(ident64, [64, 4, 64], 1),
                           bc(d1[:, ch, :], [64, 4, 64], 2), ALU.mult), r=["d1"], w=["RH1_d"])
            S.op("pe", MM(g.bank(4)[0:64, 0:256], L2, RH1[:, :]), r=["RH1_d", "RH1_m"], w=[("ps", 4)])
            S.op("dve", RED(mx[:, :], g.bank(4)[0:64, 0:256].rearrange("p (h f) -> p h f", f=64), ALU.max), r=[("ps", 4)], w=["mx"])
            S.op("dve", TT(mm2[:, 4:8], mx[:, :], mprev[:, :], ALU.max), r=["mx", "mprev"], w=["Mx"])
            S.op("dve", TT(mm2[:, 0:4], Fc[:, ch, :], mm2[:, 4:8], ALU.add), r=["Fc", "Mx"], w=["mrow"])
            S.op("dve", TT(tv[:, :], mprev[:, :], mm2[:, 4:8], ALU.subtract), r=["mprev", "Mx"], w=["tv"])
            S.op("act", ACT(av[:, :], tv[:, :], AF.Exp), r=["tv"], w=["av"])
            S.op("act", ACT(enm[:, :], mm2[:, 0:4], AF.Exp, scale=-1.0), r=["mrow"], w=["enm"])
            S.op("dve", TSC(negMx[:, :], mm2[:, 4:8], -1.0, ALU.mult), r=["Mx"], w=["negMx"])
            S.op("pe", MM(b2[0:64, 208:216], cst("LASTSEL" + md, 64), mm2[:, :]), r=["mrow", "Mx"], w=[("ps", 2)])
            if not sample:
                S.op("dve", TSC(rdec[:, 0:4], tv[:, :], cst("LASTOHp", 64)[:, 0:1], ALU.mult), r=["tv"], w=["rdec"])
            else:
                S.op("dve", TT(rdec[:, :].rearrange("p (s h) -> p s h", h=4), bc(tv[:, :], [64, NSS, 4], 1),
                               bc(cst("LASTOHs", 64), [64, NSS, 4], 2), ALU.mult), r=["tv"], w=["rdec"])
            S.op("pe", MM(b2[0:64, 256:256 + NS * 4], ones64[:, 0:64], rdec[:, 0:NS * 4]), r=["rdec"], w=[("ps", 2)])
            S.op("act", ACT(dec[:, 0:NS * 4], b2[0:64, 256:256 + NS * 4], AF.Exp), r=[("ps", 2)], w=["dec"])
            if (ti == 3 and ch == nch - 1) or sample:
                S.op("pe", MM(b2[0:NS, 220:224], cst("LASTOH" + md, 64), mm2[:, 0:4]), r=["mrow"], w=[("ps", 2)])
                S.op("act", ACT(mo[0:NS, :], b2[0:NS, 220:224], AF.Copy), r=[("ps", 2)], w=["mo"])
                if sample:
                    S.op("sp", DMA(g.ms[l], mo[0:NSS, :]), r=["mo"], dma=True)
                else:
                    S.op("sp", DMA(g.mp[l:l + 1, :], mo[0:1, :]), r=["mo"], dma=True)
            S.op("dve", TT(wC[:, :], d1[:, ch, :], b2[0:64, 212:216], ALU.subtract), r=["d1", ("ps", 2)], w=["wC"])
            S.op("act", ACT(wC[:, :], wC[:, :], AF.Exp), r=["wC"], w=["wC"])
            S.op("act", ACT(mprev[:, :], b2[0:64, 208:212], AF.Copy), r=[("ps", 2), "tv", "Mx"], w=["mprev"])
            for h in range(4):
                S.op("pe", MM(g.bank(3)[0:64, h * 64:(h + 1) * 64], kT[:, h, q0:q0 + 64], qT[:, h, q0:q0 + 64]),
                     r=[("qkT", 0, h), ("qkT", 1, h)], w=[("ps", 3)])
            S.op("dve", TT(RH2[0:64, :].rearrange("p (h f) -> p h f", f=64), bc(ident64, [64, 4, 64], 1),
                           bc(negMx[:, :], [64, 4, 64], 2), ALU.mult), r=["negMx"], w=["RH2_d"])
            S.op("pe", MM(g.bank(4)[0:64, 256:512], L2, RH2[:, :]), r=["RH2_d", "RH2_m"], w=[("ps", 4)])
            S.op("dve", TT(X2[:, :].rearrange("p (h f) -> p h f", f=64), g.bank(4)[0:64, 256:512].rearrange("p (h f) -> p h f", f=64),
                           bc(d1[:, ch, :], [64, 4, 64], 2), ALU.add), r=[("ps", 4), "d1"], w=["X2"])
            S.op("act", ACT(X2[:, :], X2[:, :], AF.Exp), r=["X2"], w=["X2"])
            S.op("dve", TT(SmT[:, :, :].rearrange("p h f -> p (h f)"), X2[:, :], g.bank(3)[0:64, 0:256], ALU.mult),
                 r=["X2", ("ps", 3)], w=["SmT"])
            for h in range(4):
                S.op("pe", MM(g.bank(5)[0:64, h * 128:(h + 1) * 128], SmT[:, h, :], vtm[:, ch, h, :]), r=["SmT", ("vtm", ch)], w=[("ps", 5)])
            for h in range(4):
                S.op("pe", MM(b2[0:64, 240 + h:241 + h], SmT[:, h, :], g.onesb[0:64, 0:1]), r=["SmT"], w=[("ps", 2)])
            S.op("act", ACT(P2s[:, :, :], g.bank(5)[0:64, :].rearrange("p (h v) -> p h v", h=4), AF.Copy), r=[("ps", 5)], w=["P2s"])
            S.op("pool", TT(kw[:, :, :], ktm[:, ch, :, :], bc(wC[:, :], [64, 4, 64], 2), ALU.mult), r=[("ktm", ch), "wC"], w=["kw"])
            if not sample:
                for h in range(4):
                    S.op("pe", MM(g.bank(7)[0:64, h * 128:(h + 1) * 128], qT[:, h, q0:q0 + 64], Cb[:, h, :]),
                         r=[("qkT", 0, h), "Cb"], w=[("ps", 7)])
                for h in range(4):
                    S.op("pe", MM(b2[0:64, 244 + h:245 + h], qT[:, h, q0:q0 + 64], nb[:, h:h + 1]), r=[("qkT", 0, h), "nb"], w=[("ps", 2)])
                S.op("dve", TT(C32[:, :, :], C32[:, :, :], bc(dec[:, 0:4], [64, 4, 128], 2), ALU.mult), r=["C32", "dec"], w=["C32"])
                for h in range(4):
                    S.op("pe", MM(g.bank(3)[0:64, h * 128:(h + 1) * 128], kw[:, h, :], vtm[:, ch, h, :]), r=["kw", ("vtm", ch)], w=[("ps", 3)])
                for h in range(4):
                    S.op("pe", MM(b2[0:64, 248 + h:249 + h], kw[:, h, :], g.onesb[0:64, 0:1]), r=["kw"], w=[("ps", 2)])
                S.op("dve", TT(Cb[:, :, :], C32[:, :, :], g.bank(3)[0:64, :].rearrange("p (h v) -> p h v", h=4), ALU.add),
                     r=["C32", ("ps", 3)], w=["Cb"])
                S.op("dve", TT(C32[:, :, :], C32[:, :, :], g.bank(3)[0:64, :].rearrange("p (h v) -> p h v", h=4), ALU.add),
                     r=["C32", ("ps", 3)], w=["C32"])
                S.op("dve", TT(n32[:, :], n32[:, :], dec[:, 0:4], ALU.mult), r=["n32", "dec"], w=["n32"])
                S.op("dve", TT(n32[:, :], n32[:, :], b2[0:64, 248:252], ALU.add), r=["n32", ("ps", 2)], w=["n32"])
                S.op("dve", CP(nb[:, :], n32[:, :]), r=["n32"], w=["nb"])
            else:
                nnew = A2.alloc([64, NSS, 4], F32, "nnew")
                for h in range(4):
                    sl = h % 2
                    S.op("pool", TT(qTm[sl][:, :, :], bc(qT[:, h, 0:64], [64, NSS, 64], 1), smf, ALU.mult),
                         r=[("qkT", 0, h)], w=[("qTm", sl)])
                    S.op("pool", TT(kwm[sl][:, :, :], bc(kw[:, h, :], [64, NSS, 64], 1), bc(cst("SEQOHs", 64), [64, NSS, 64], 2),
                                    ALU.mult), r=["kw"], w=[("kwm", sl)])
                    src = g.C0[l, :, h, :, :].rearrange("s k v -> k s v")
                    S.op("sp", DMA(C0h[sl][:], src), w=[("C0h", sl)], dma=True)
                    wload(g, Cbh[sl][:], src, ("Cbh", sl))
                    for s_ in range(NSS):
                        S.op("pe", MM(g.bank(7)[0:64, h * 128:(h + 1) * 128], qTm[sl][:, s_, :], Cbh[sl][:, s_, :],
                                      start=(s_ == 0), stop=(s_ == NSS - 1)), r=[("qTm", sl), ("Cbh", sl)], w=[("ps", 7)])
                    for s_ in range(NSS):
                        S.op("pe", MM(b2[0:64, 244 + h:245 + h], qTm[sl][:, s_, :], n0b[:, s_, h:h + 1],
                                      start=(s_ == 0), stop=(s_ == NSS - 1)), r=[("qTm", sl), "n0b"], w=[("ps", 2)])
                    for grp in range(4):
                        pbk = grp % 2
                        for s4 in range(4):
                            s_ = grp * 4 + s4
                            S.op("pe", MM(g.bank(pbk)[0:64, s4 * 128:(s4 + 1) * 128], kwm[sl][:, s_, :], vtm[:, 0, h, :]),
                                 r=[("kwm", sl), ("vtm", 0)], w=[("ps", pbk)])
                        for s4 in range(4):
                            s_ = grp * 4 + s4
                            S.op("dve", STT(C0h[sl][:, s_, :], C0h[sl][:, s_, :], dec[:, s_ * 4 + h:s_ * 4 + h + 1],
                                            g.bank(pbk)[0:64, s4 * 128:(s4 + 1) * 128], ALU.mult, ALU.add),
                                 r=[("C0h", sl), "dec", ("ps", pbk)], w=[("C0h", sl)])
                    S.op("sp", DMA(g.Cs[l, :, h, :, :].rearrange("s k v -> k s v"), C0h[sl][:]), r=[("C0h", sl)], dma=True)
                    for s_ in range(NSS):
                        S.op("pe", MM(b2[0:64, 384 + h * 16 + s_:385 + h * 16 + s_], kwm[sl][:, s_, :], g.onesb[0:64, 0:1]),
                             r=[("kwm", sl)], w=[("ps", 2)])
                S.op("dve", TT(nnew[:, :, :], n0t[:, :, :], dec[:, 0:64].rearrange("p (s h) -> p s h", h=4), ALU.mult),
                     r=["n0t", "dec"], w=["nnew"])
                S.op("dve", TT(nnew[:, :, :], nnew[:, :, :], b2[0:64, 384:448].rearrange("p (h s) -> p s h", h=4), ALU.add),
                     r=["nnew", ("ps", 2)], w=["nnew"])
                S.op("pe", TR(g.bank(3)[0:64, 0:64], nnew[:, :, :].rearrange("p s h -> p (s h)"), ident64), r=["nnew"], w=[("ps", 3)])
                S.op("act", ACT(nio[:, :], g.bank(3)[0:64, 0:64], AF.Copy), r=[("ps", 3)], w=["nio"])
                S.op("sp", DMA(g.ns[l].rearrange("s h k -> (s h) k"), nio[:, :]), r=["nio"], dma=True)
            S.op("dve", TT(num[:, :, :], g.bank(7)[0:64, :].rearrange("p (h v) -> p h v", h=4), bc(av[:, :], [64, 4, 128], 2), ALU.mult),
                 r=[("ps", 7), "av"], w=["num"])
            S.op("dve", TT(num[:, :, :], num[:, :, :], P2s[:, :, :], ALU.add), r=["num", "P2s"], w=["num"])
            S.op("dve", TT(dd[:, :], av[:, :], b2[0:64, 244:248], ALU.mult), r=["av", ("ps", 2)], w=["dd"])
            S.op("dve", TT(dd[:, :], dd[:, :], b2[0:64, 240:244], ALU.add), r=["dd", ("ps", 2)], w=["dd"])
            S.op("dve", TSC(rden[:, :], dd[:, :], -1.0, ALU.mult), r=["dd"], w=["rden"])
            S.op("dve", TT(dd[:, :], dd[:, :], rden[:, :], ALU.max), r=["dd", "rden"], w=["dd"])
            S.op("dve", TT(dd[:, :], dd[:, :], enm[:, :], ALU.max), r=["dd", "enm"], w=["dd"])
            S.op("dve", RCP(rden[:, :], dd[:, :]), r=["dd"], w=["rden"])
            for h in range(4):
                S.op("act", ACT(X2[:, 0:128], num[:, h, :], AF.Square, accum=ssn[:, h:h + 1]), r=["num"], w=["X2", ("ssn", h)])
            ssr = [("ssn", h) for h in range(4)]
            S.op("dve", TT(ssn[:, :], ssn[:, :], rden[:, :], ALU.mult), r=ssr + ["rden"], w=ssr)
            S.op("dve", TT(ssn[:, :], ssn[:, :], rden[:, :], ALU.mult), r=ssr + ["rden"], w=ssr)
            S.op("act", ACT(ssn[:, :], ssn[:, :], AF.Ln, bias=g.epsb[0:64, 0:1], scale=1.0 / 128), r=ssr, w=ssr)
            S.op("act", ACT(ssn[:, :], ssn[:, :], AF.Exp, scale=-0.5), r=ssr, w=ssr)
            S.op("dve", TT(ssn[:, :], ssn[:, :], rden[:, :], ALU.mult), r=ssr + ["rden"], w=ssr)
            S.op("pool", TT(hn[:, :, :], num[:, :, :], bc(ssn[:, :], [64, 4, 128], 2), ALU.mult), r=["num"] + ssr, w=["hn"])
            for h in range(4):
                S.op("pe", TR(b6[:, h * 64:(h + 1) * 64], hn[:, h, :], g.identb[0:64, 0:64]), r=["hn"], w=[("ps", 6)])
            S.op("dve", TT(g.mixT[:, 4:8, t0 + q0:t0 + q0 + 64], b6[:, 0:256].rearrange("p (h f) -> p h f", h=4),
                           bc(g.pvt[:, pvo + PV_MLNORM:pvo + PV_MLNORM + 4], [128, 4, 64], 2), ALU.mult), r=[("ps", 6)],
                 w=[("mixT", 4 + c_, ti) for c_ in range(4)])
        if ti == 3:
            S.op("sp", DMA(g.Cp[l].rearrange("h k v -> k h v"), C32[:, :, :]), r=["C32"], dma=True)
            S.op("pe", TR(g.bank(3)[0:4, 0:64], n32[:, :], ident64), r=["n32"], w=[("ps", 3)])
            S.op("act", ACT(nio[0:4, :], g.bank(3)[0:4, 0:64], AF.Copy), r=[("ps", 3)], w=["nio"])
            S.op("sp", DMA(g.np_[l], nio[0:4, :]), r=["nio"], dma=True)
    S.barrier()
    A.release(m)
```
